# Optimizing a Trainium2 kernel written in Bass

```python
import math
import jax, jax.numpy as jnp
from jax import lax
import numpy as np

D_MODEL = 4096
BATCH = 1
SEQ = 8192
DEPTH = 2

N_A_LAYERS = DEPTH // 2
N_B_LAYERS = DEPTH - N_A_LAYERS
D_FF = 11008
SSM_EXPAND = 2
D_INNER = SSM_EXPAND * D_MODEL
SSM_HEAD_DIM = 64
SSM_HEADS = D_INNER // SSM_HEAD_DIM
SSM_GROUPS = 8
SSM_HPG = SSM_HEADS // SSM_GROUPS
SSM_STATE = 128
CONV_WIDTH = 4
CHUNK = 128
ATTN_HEADS = 32
KV_GROUPS = 4
Q_PER_KV = ATTN_HEADS // KV_GROUPS
HEAD_DIM = 128
CMP_BLOCK = 32
CMP_STRIDE = 16
SEL_BLOCK = 64
N_SELECT = 16
WINDOW = 512
Q_BLOCK = 128
ROPE_THETA = 500000.0
ROPE_DIM = HEAD_DIM // 4
EPS = 1e-6
NEG = -1e30
FORCE_BONUS = 1e4

kernel_name = 'yoco_ssd_nsa_macaron'


def rms_norm(x, g):
    xf = x.astype(jnp.float32)
    y = xf * lax.rsqrt(jnp.mean(xf * xf, axis=-1, keepdims=True) + EPS)
    return (y * g.astype(jnp.float32)).astype(x.dtype)


def swiglu(x, w_in, w_out):
    a, b = jnp.split(x @ w_in, 2, axis=-1)
    return (jax.nn.silu(a) * b) @ w_out


def partial_rope(x, pos):
    half = ROPE_DIM // 2
    inv = ROPE_THETA ** (-jnp.arange(half, dtype=jnp.float32) / half)
    ang = pos.astype(jnp.float32)[:, None] * inv[None, :]
    cos = jnp.cos(ang)[None, :, None, :]
    sin = jnp.sin(ang)[None, :, None, :]
    xf = x.astype(jnp.float32)
    x1 = xf[..., :half]
    x2 = xf[..., half:ROPE_DIM]
    out = jnp.concatenate([x1 * cos - x2 * sin, x2 * cos + x1 * sin, xf[..., ROPE_DIM:]], axis=-1)
    return out.astype(x.dtype)


def ssd_chunked(xdt, a, bm, cm):
    b, s, g, r, p = xdt.shape
    n = bm.shape[-1]
    nc = s // CHUNK

    def to_chunks(t):
        return jnp.moveaxis(t.reshape((b, nc, CHUNK) + t.shape[2:]), 1, 0)

    causal = jnp.tril(jnp.ones((CHUNK, CHUNK), dtype=bool))[None, :, :, None, None]

    def step(state, inp):
        xc, ac, bc, cc = inp
        cum = jnp.cumsum(ac, axis=1)
        seg = cum[:, :, None] - cum[:, None, :]
        decay = jnp.exp(jnp.where(causal, seg, -jnp.inf))
        cb = jnp.einsum('blgn,bsgn->blsg', cc, bc)
        y_diag = jnp.einsum('blsgr,bsgrp->blgrp', cb[..., None] * decay, xc)
        y_off = jnp.einsum('blgn,bgrpn->blgrp', cc, state) * jnp.exp(cum)[..., None]
        to_end = jnp.exp(cum[:, -1:] - cum)
        new_state = state * jnp.exp(cum[:, -1])[..., None, None] + jnp.einsum(
            'bsgn,bsgrp->bgrpn', bc, xc * to_end[..., None])
        return new_state, y_diag + y_off

    state0 = jnp.zeros((b, g, r, p, n), jnp.float32)
    _, y = lax.scan(step, state0, (to_chunks(xdt), to_chunks(a), to_chunks(bm), to_chunks(cm)))
    return jnp.moveaxis(y, 0, 1).reshape(b, s, g, r, p)


def mamba2_mixer(u, w_in, conv_w, conv_b, dt_bias, a_log, d_skip, norm_g, w_out):
    b, s, _ = u.shape
    gn = SSM_GROUPS * SSM_STATE
    conv_ch = D_INNER + 2 * gn
    zxbcdt = u @ w_in
    z = zxbcdt[..., :D_INNER]
    xbc = zxbcdt[..., D_INNER:D_INNER + conv_ch]
    dt = zxbcdt[..., D_INNER + conv_ch:]
    xbc = lax.conv_general_dilated(
        xbc, conv_w[:, None, :], window_strides=(1,), padding=[(CONV_WIDTH - 1, 0)],
        dimension_numbers=('NWC', 'WIO', 'NWC'), feature_group_count=conv_ch) + conv_b
    xbc = jax.nn.silu(xbc)
    xs = xbc[..., :D_INNER].reshape(b, s, SSM_GROUPS, SSM_HPG, SSM_HEAD_DIM).astype(jnp.float32)
    bm = xbc[..., D_INNER:D_INNER + gn].reshape(b, s, SSM_GROUPS, SSM_STATE).astype(jnp.float32)
    cm = xbc[..., D_INNER + gn:].reshape(b, s, SSM_GROUPS, SSM_STATE).astype(jnp.float32)
    dt = jax.nn.softplus(dt.astype(jnp.float32) + dt_bias.astype(jnp.float32))
    dt = dt.reshape(b, s, SSM_GROUPS, SSM_HPG)
    a = -jnp.exp(a_log.astype(jnp.float32)).reshape(SSM_GROUPS, SSM_HPG)
    y = ssd_chunked(xs * dt[..., None], dt * a, bm, cm)
    y = y + d_skip.astype(jnp.float32).reshape(SSM_GROUPS, SSM_HPG)[..., None] * xs
    y = y.reshape(b, s, D_INNER) * jax.nn.silu(z.astype(jnp.float32))
    yg = y.reshape(b, s, SSM_GROUPS, D_INNER // SSM_GROUPS)
    yg = yg * lax.rsqrt(jnp.mean(yg * yg, axis=-1, keepdims=True) + EPS)
    y = yg.reshape(b, s, D_INNER) * norm_g.astype(jnp.float32)
    return y.astype(u.dtype) @ w_out


def shared_kv(h, kv_norm, kv_w, cmp_pe_k, cmp_w1_k, cmp_w2_k, cmp_pe_v, cmp_w1_v, cmp_w2_v,
              k_norm_cmp, k_norm_slc, k_norm_win):
    b, s, _ = h.shape
    kv = (rms_norm(h, kv_norm) @ kv_w).reshape(b, s, 6, KV_GROUPS, HEAD_DIM)
    k_c, v_c, k_s, v_s, k_w, v_w = (kv[:, :, i] for i in range(6))
    pos = jnp.arange(s)
    n_cmp = (s - CMP_BLOCK) // CMP_STRIDE + 1
    idx = jnp.arange(n_cmp)[:, None] * CMP_STRIDE + jnp.arange(CMP_BLOCK)[None, :]

    def compress(t, pe, w1, w2):
        blk = t[:, idx] + pe[None, None, :, None, :]
        blk = jnp.moveaxis(blk, 3, 2).reshape(b, n_cmp, KV_GROUPS, CMP_BLOCK * HEAD_DIM)
        return jax.nn.silu(blk @ w1) @ w2

    kc = partial_rope(rms_norm(compress(k_c, cmp_pe_k, cmp_w1_k, cmp_w2_k), k_norm_cmp), idx[:, -1])
    vc = compress(v_c, cmp_pe_v, cmp_w1_v, cmp_w2_v)
    ks = partial_rope(rms_norm(k_s, k_norm_slc), pos)
    kw = partial_rope(rms_norm(k_w, k_norm_win), pos)
    return kc, vc, ks, v_s, kw, v_w


def nsa_mixer(u, w_qg, q_norm_g, w_o, kc, vc, ks, vs, kw, vw):
    b, s, _ = u.shape
    qg = u @ w_qg
    q = qg[..., :ATTN_HEADS * HEAD_DIM].reshape(b, s, ATTN_HEADS, HEAD_DIM)
    gates = jax.nn.sigmoid(qg[..., ATTN_HEADS * HEAD_DIM:].astype(jnp.float32))
    gates = gates.reshape(b, s, KV_GROUPS, Q_PER_KV, 3)
    q = partial_rope(rms_norm(q, q_norm_g), jnp.arange(s)).reshape(b, s, KV_GROUPS, Q_PER_KV, HEAD_DIM)
    scale = HEAD_DIM ** -0.5

    n_cmp = kc.shape[1]
    cmp_end = jnp.arange(n_cmp) * CMP_STRIDE + CMP_BLOCK - 1
    n_sblk = s // SEL_BLOCK
    n_top = min(N_SELECT, n_sblk)
    ratio = SEL_BLOCK // CMP_STRIDE
    offs = np.arange(-(CMP_BLOCK // CMP_STRIDE), ratio + 1)
    ov = np.clip(np.minimum(offs * CMP_STRIDE + CMP_BLOCK, SEL_BLOCK) - np.maximum(offs * CMP_STRIDE, 0), 0, None)
    keep = ov > 0
    offs, ov = offs[keep], ov[keep]
    sel_w = jnp.asarray(ov / CMP_BLOCK, dtype=jnp.float32)
    pad_l = int(-offs.min())
    sel_idx = np.arange(n_sblk)[:, None] * ratio + offs[None, :] + pad_l
    pad_r = max(0, int(sel_idx.max()) + 1 - (n_cmp + pad_l))
    sel_idx = jnp.asarray(sel_idx, dtype=jnp.int32)

    ks_blk = jnp.moveaxis(ks.reshape(b, n_sblk, SEL_BLOCK, KV_GROUPS, HEAD_DIM), 3, 1)
    vs_blk = jnp.moveaxis(vs.reshape(b, n_sblk, SEL_BLOCK, KV_GROUPS, HEAD_DIM), 3, 1)
    bi = jnp.arange(b)[:, None, None, None]
    gi = jnp.arange(KV_GROUPS)[None, :, None, None]
    sblk_ids = jnp.arange(n_sblk)
    win_len = WINDOW + Q_BLOCK
    kw_pad = jnp.pad(kw, ((0, 0), (WINDOW, 0), (0, 0), (0, 0)))
    vw_pad = jnp.pad(vw, ((0, 0), (WINDOW, 0), (0, 0), (0, 0)))

    def block(args):
        qb_idx, qblk, gblk = args
        t = qb_idx * Q_BLOCK + jnp.arange(Q_BLOCK)
        sc = jnp.einsum('bqgrd,bcgd->bgrqc', qblk, kc).astype(jnp.float32) * scale
        mc = cmp_end[None, :] <= t[:, None]
        pc = jax.nn.softmax(jnp.where(mc, sc, NEG), axis=-1) * mc
        o_c = jnp.einsum('bgrqc,bcgd->bqgrd', pc.astype(vc.dtype), vc)
        imp = jnp.pad(pc.sum(axis=2), ((0, 0), (0, 0), (0, 0), (pad_l, pad_r)))
        imp = jnp.einsum('bgqjo,o->bgqj', imp[..., sel_idx], sel_w)
        cur = t // SEL_BLOCK
        valid = sblk_ids[None, :] <= cur[:, None]
        forced = (sblk_ids[None, :] == 0) | (sblk_ids[None, :] == cur[:, None]) | (sblk_ids[None, :] == cur[:, None] - 1)
        score = jnp.where(valid, imp + jnp.where(forced, FORCE_BONUS, 0.0), NEG)
        _, top = lax.top_k(score, n_top)
        k_g = ks_blk[bi, gi, top]
        v_g = vs_blk[bi, gi, top]
        tok = top[..., None] * SEL_BLOCK + jnp.arange(SEL_BLOCK)
        ms = tok <= t[None, None, :, None, None]
        ss = jnp.einsum('bqgrd,bgqnkd->bgrqnk', qblk, k_g).astype(jnp.float32) * scale
        ss = jnp.where(ms[:, :, None], ss, NEG).reshape(b, KV_GROUPS, Q_PER_KV, Q_BLOCK, n_top * SEL_BLOCK)
        ps = jax.nn.softmax(ss, axis=-1).reshape(b, KV_GROUPS, Q_PER_KV, Q_BLOCK, n_top, SEL_BLOCK)
        o_s = jnp.einsum('bgrqnk,bgqnkd->bqgrd', ps.astype(v_g.dtype), v_g)
        start = qb_idx * Q_BLOCK
        kwin = lax.dynamic_slice_in_dim(kw_pad, start, win_len, axis=1)
        vwin = lax.dynamic_slice_in_dim(vw_pad, start, win_len, axis=1)
        spos = start - WINDOW + jnp.arange(win_len)
        mw = (spos[None, :] <= t[:, None]) & (spos[None, :] > t[:, None] - WINDOW) & (spos[None, :] >= 0)
        sw = jnp.einsum('bqgrd,bkgd->bgrqk', qblk, kwin).astype(jnp.float32) * scale
        pw = jax.nn.softmax(jnp.where(mw, sw, NEG), axis=-1)
        o_w = jnp.einsum('bgrqk,bkgd->bqgrd', pw.astype(vwin.dtype), vwin)
        o = gblk[..., 0:1] * o_c + gblk[..., 1:2] * o_s + gblk[..., 2:3] * o_w
        return o.astype(u.dtype)

    nqb = s // Q_BLOCK
    q_blocks = jnp.moveaxis(q.reshape(b, nqb, Q_BLOCK, KV_GROUPS, Q_PER_KV, HEAD_DIM), 1, 0)
    g_blocks = jnp.moveaxis(gates.reshape(b, nqb, Q_BLOCK, KV_GROUPS, Q_PER_KV, 3), 1, 0)
    o = lax.map(block, (jnp.arange(nqb), q_blocks, g_blocks))
    o = jnp.moveaxis(o, 0, 1).reshape(b, s, ATTN_HEADS * HEAD_DIM)
    return o @ w_o


def setup_inputs(seed: int = 0) -> dict:
    key = jax.random.key(seed)
    ks = jax.random.split(key, 32)
    f32 = jnp.float32

    def nrm(k, shape, fan_in):
        return jax.random.normal(k, shape, f32) * (fan_in ** -0.5)

    def gain(k, shape):
        return 1.0 + 0.02 * jax.random.normal(k, shape, f32)

    gn = SSM_GROUPS * SSM_STATE
    conv_ch = D_INNER + 2 * gn
    in_cols = 2 * D_INNER + 2 * gn + SSM_HEADS
    dt0 = jnp.exp(jax.random.uniform(ks[10], (N_A_LAYERS, SSM_HEADS), f32, math.log(1e-3), math.log(1e-1)))
    return {
        'x': jax.random.normal(ks[0], (BATCH, SEQ, D_MODEL), f32),
        'ffn_a_norm': gain(ks[1], (DEPTH, D_MODEL)),
        'ffn_a_w_in': nrm(ks[2], (DEPTH, D_MODEL, 2 * D_FF), D_MODEL),
        'ffn_a_w_out': nrm(ks[3], (DEPTH, D_FF, D_MODEL), D_FF),
        'ffn_b_norm': gain(ks[4], (DEPTH, D_MODEL)),
        'ffn_b_w_in': nrm(ks[5], (DEPTH, D_MODEL, 2 * D_FF), D_MODEL),
        'ffn_b_w_out': nrm(ks[6], (DEPTH, D_FF, D_MODEL), D_FF),
        'mix_norm': gain(ks[7], (DEPTH, D_MODEL)),
        'ssm_w_in': nrm(ks[8], (N_A_LAYERS, D_MODEL, in_cols), D_MODEL),
        'ssm_conv_w': nrm(ks[9], (N_A_LAYERS, CONV_WIDTH, conv_ch), CONV_WIDTH),
        'ssm_conv_b': 0.02 * jax.random.normal(ks[11], (N_A_LAYERS, conv_ch), f32),
        'ssm_dt_bias': dt0 + jnp.log(-jnp.expm1(-dt0)),
        'ssm_a_log': jnp.log(jax.random.uniform(ks[12], (N_A_LAYERS, SSM_HEADS), f32, 1.0, 16.0)),
        'ssm_d': gain(ks[13], (N_A_LAYERS, SSM_HEADS)),
        'ssm_norm': gain(ks[14], (N_A_LAYERS, D_INNER)),
        'ssm_w_out': nrm(ks[15], (N_A_LAYERS, D_INNER, D_MODEL), D_INNER),
        'kv_norm': gain(ks[16], (D_MODEL,)),
        'kv_w': nrm(ks[17], (D_MODEL, 6 * KV_GROUPS * HEAD_DIM), D_MODEL),
        'cmp_pe_k': 0.02 * jax.random.normal(ks[18], (CMP_BLOCK, HEAD_DIM), f32),
        'cmp_w1_k': nrm(ks[19], (CMP_BLOCK * HEAD_DIM, HEAD_DIM), CMP_BLOCK * HEAD_DIM),
        'cmp_w2_k': nrm(ks[20], (HEAD_DIM, HEAD_DIM), HEAD_DIM),
        'cmp_pe_v': 0.02 * jax.random.normal(ks[21], (CMP_BLOCK, HEAD_DIM), f32),
        'cmp_w1_v': nrm(ks[22], (CMP_BLOCK * HEAD_DIM, HEAD_DIM), CMP_BLOCK * HEAD_DIM),
        'cmp_w2_v': nrm(ks[23], (HEAD_DIM, HEAD_DIM), HEAD_DIM),
        'k_norm_cmp': gain(ks[24], (HEAD_DIM,)),
        'k_norm_slc': gain(ks[25], (HEAD_DIM,)),
        'k_norm_win': gain(ks[26], (HEAD_DIM,)),
        'attn_w_qg': nrm(ks[27], (N_B_LAYERS, D_MODEL, ATTN_HEADS * HEAD_DIM + 3 * ATTN_HEADS), D_MODEL),
        'attn_q_norm': gain(ks[28], (N_B_LAYERS, HEAD_DIM)),
        'attn_w_o': nrm(ks[29], (N_B_LAYERS, ATTN_HEADS * HEAD_DIM, D_MODEL), ATTN_HEADS * HEAD_DIM),
    }


def reference(x, ffn_a_norm, ffn_a_w_in, ffn_a_w_out, ffn_b_norm, ffn_b_w_in, ffn_b_w_out, mix_norm,
              ssm_w_in, ssm_conv_w, ssm_conv_b, ssm_dt_bias, ssm_a_log, ssm_d, ssm_norm, ssm_w_out,
              kv_norm, kv_w, cmp_pe_k, cmp_w1_k, cmp_w2_k, cmp_pe_v, cmp_w1_v, cmp_w2_v,
              k_norm_cmp, k_norm_slc, k_norm_win, attn_w_qg, attn_q_norm, attn_w_o):
    h = x
    kv_shared = None
    for i in range(DEPTH):
        h = h + 0.5 * swiglu(rms_norm(h, ffn_a_norm[i]), ffn_a_w_in[i], ffn_a_w_out[i])
        u = rms_norm(h, mix_norm[i])
        if i < N_A_LAYERS:
            h = h + mamba2_mixer(u, ssm_w_in[i], ssm_conv_w[i], ssm_conv_b[i], ssm_dt_bias[i],
                                 ssm_a_log[i], ssm_d[i], ssm_norm[i], ssm_w_out[i])
        else:
            j = i - N_A_LAYERS
            kc, vc, ks, vs, kw, vw = kv_shared
            h = h + nsa_mixer(u, attn_w_qg[j], attn_q_norm[j], attn_w_o[j], kc, vc, ks, vs, kw, vw)
        h = h + 0.5 * swiglu(rms_norm(h, ffn_b_norm[i]), ffn_b_w_in[i], ffn_b_w_out[i])
        if i == N_A_LAYERS - 1:
            kv_shared = shared_kv(h, kv_norm, kv_w, cmp_pe_k, cmp_w1_k, cmp_w2_k, cmp_pe_v, cmp_w1_v,
                                  cmp_w2_v, k_norm_cmp, k_norm_slc, k_norm_win)
    return h
```

```python
import contextlib
import numpy as np
import concourse.bass as bass
import concourse.mybir as mybir
from concourse.bass_utils import run_bass_kernel_spmd

F32 = mybir.dt.float32
BF16 = mybir.dt.bfloat16
I32 = mybir.dt.int32
AF = mybir.ActivationFunctionType
ALU = mybir.AluOpType
AX = mybir.AxisListType

D = 4096
DFF = 11008
KC = D // 128
JC = DFF // 128
NCORE = 8
EPS = 1e-6
SSM_COLS = 18560


class Buf:
    def __init__(self, h, name, sem=None):
        self.h = h
        self.name = name
        self.writes = {}
        self.reads = {}
        self.sem = sem
        self.semval = 0

    def __getitem__(self, idx):
        return self.h[idx]

    def ap(self):
        return self.h.ap()


def _merge(d, s):
    for k, v in s.items():
        if k not in d or d[k][1] < v[1]:
            d[k] = v


class Prog:
    ENGS = ["pe", "act", "dve", "pool", "sp"]

    def __init__(self, nc):
        self.nc = nc
        self.stack = contextlib.ExitStack()
        self.q = {e: [] for e in self.ENGS}
        self.cnt = {e: 0 for e in self.ENGS}
        self.esem = {}
        for e in self.ENGS:
            self.esem[e] = self.stack.enter_context(nc.semaphore("es_" + e))
        self.known = {e: {} for e in self.ENGS}
        self.nbuf = 0

    def sbuf(self, shape, dtype, name=None, dma=False):
        self.nbuf += 1
        name = name or f"sb{self.nbuf}"
        h = self.stack.enter_context(self.nc.sbuf_tensor(name, list(shape), dtype))
        return self.track(h, name, dma)

    def psum(self, shape, dtype, name=None):
        self.nbuf += 1
        name = name or f"ps{self.nbuf}"
        h = self.stack.enter_context(self.nc.psum_tensor(name, list(shape), dtype))
        return Buf(h, name)

    def dram(self, name, shape, dtype, kind="Internal"):
        h = self.nc.dram_tensor(name, list(shape), dtype, kind=kind)
        return self.track(h, name, True)

    def track(self, h, name, dma=False):
        sem = None
        if dma:
            sem = self.stack.enter_context(self.nc.semaphore("ds_" + name))
        return Buf(h, name, sem)

    def op(self, eng, fn, reads=(), writes=(), dma_dst=None, nowaw=False):
        waits = {}
        for b in reads:
            _merge(waits, b.writes)
        for b in writes:
            if not nowaw:
                _merge(waits, b.writes)
            _merge(waits, b.reads)
        if dma_dst is not None:
            sem = dma_dst.sem
            assert sem is not None, dma_dst.name
            dma_dst.semval += 16
            tok = (sem, dma_dst.semval)
            inc = 16
        else:
            self.cnt[eng] += 1
            sem = self.esem[eng]
            tok = (sem, self.cnt[eng])
            inc = 1
        wl = []
        kn = self.known[eng]
        for k, (s, v) in waits.items():
            if eng == "pe" and dma_dst is None and s is self.esem["pe"]:
                continue
            if kn.get(k, 0) >= v:
                continue
            kn[k] = v
            wl.append((s, v))
        self.q[eng].append((wl, fn, sem, inc))
        key = sem.num
        for b in reads:
            if b.reads.get(key, (None, 0))[1] < tok[1]:
                b.reads[key] = tok
        for b in writes:
            if nowaw:
                if b.writes.get(key, (None, 0))[1] < tok[1]:
                    b.writes[key] = tok
            else:
                b.writes = {key: tok}
                b.reads = {}
        return tok

    def dma(self, eng, dst_buf, dst_ap, src_buf, src_ap, nowaw=True, **kw):
        def fn(e):
            return e.dma_start(out=dst_ap, in_=src_ap, **kw)
        return self.op(eng, fn, reads=[src_buf], writes=[dst_buf], dma_dst=dst_buf, nowaw=nowaw)

    def wait_all(self, eng, bufs):
        waits = {}
        for b in bufs:
            _merge(waits, b.writes)
        wl = [(s, v) for k, (s, v) in waits.items()]
        self.q[eng].append((wl, None, None, 0))

    def emit(self):
        nc = self.nc
        q = self.q

        def replay(e, lst):
            for wl, fn, sem, inc in lst:
                for s, v in wl:
                    e.wait_ge(s, v)
                if fn is not None:
                    ins = fn(e)
                    ins.then_inc(sem, inc)

        with nc.Block() as block:
            @block.tensor
            def _(e):
                replay(e, q["pe"])

            @block.scalar
            def _(e):
                replay(e, q["act"])

            @block.vector
            def _(e):
                replay(e, q["dve"])

            @block.gpsimd
            def _(e):
                replay(e, q["pool"])

            @block.sync
            def _(e):
                replay(e, q["sp"])
        self.stack.close()


def lay_w(w):
    K, M = w.shape
    return np.ascontiguousarray(w.reshape(K // 128, 128, M // 128, 128).transpose(2, 1, 0, 3))


def lay_vec(g):
    return np.ascontiguousarray(g.reshape(-1, 128).T)


class Ctx:
    def __init__(self, P, TP):
        self.TP = TP
        self.R = P.sbuf([128, JC * 512], BF16, "R", dma=True)
        self.R32 = self.R.h.ap().bitcast(F32)
        self.xn = P.sbuf([128, KC * 512], BF16, "xn")
        self.wb = [P.sbuf([128, DFF], BF16, f"wb{i}", dma=True) for i in range(2)]
        self.ps = [P.psum([128, 512], F32, f"psb{i}") for i in range(8)]
        self.ones = P.sbuf([128, 128], F32, "ones")
        self.epsT = P.sbuf([128, 1], F32, "epsT")
        self.sq = [P.sbuf([128, 512], F32, f"sq{i}") for i in range(2)]
        self.rstd = P.sbuf([128, 512], F32, "rstd")
        self.hin = [P.sbuf([128, 512], F32, f"hin{i}", dma=True) for i in range(2)]
        self.hout = [P.sbuf([128, 512], F32, f"hout{i}") for i in range(2)]
        self.sa = [P.sbuf([128, 512], F32, f"sa{i}") for i in range(2)]
        P.op("dve", lambda e: e.memset(self.ones[:, :], 1.0), writes=[self.ones])
        P.op("dve", lambda e: e.memset(self.epsT[:, :], EPS), writes=[self.epsT])
        self.wcnt = 0
        self.ecnt = 0

    def h32(self, kc, TP):
        return self.R32[:, kc * TP:(kc + 1) * TP]

    def g(self, j, TP):
        return self.R[:, j * TP:(j + 1) * TP]

    def xnc(self, kc, TP):
        return self.xn[:, kc * TP:(kc + 1) * TP]


def load_vec(P, name, dram_buf, n):
    t = P.sbuf([128, n], F32, name, dma=True)
    P.dma("sp", t, t[:, :], dram_buf, dram_buf.ap())
    return t


def emit_load_h(P, C, h_dram, tok0, TP, kcn=KC):
    hv = h_dram.ap().rearrange("(kc p) t -> p kc t", p=128)
    dst = C.R32[:, 0:kcn * TP].rearrange("p (kc t) -> p kc t", t=TP)
    step = 8
    for k0 in range(0, kcn, step):
        P.dma("sp", C.R, dst[:, k0:k0 + step, :], h_dram, hv[:, k0:k0 + step, tok0:tok0 + TP])


def emit_norm(P, C, g_tile, TP, kcn=KC):
    ps = C.ps[6]
    for kc in range(kcn):
        b = kc % 2
        P.op("act", lambda e, kc=kc, b=b: e.activation(out=C.sq[b][:, :TP], in_=C.h32(kc, TP), func=AF.Square),
             reads=[C.R], writes=[C.sq[b]])
        P.op("pe", lambda e, kc=kc, b=b: e.matmul(ps[:, :TP], lhsT=C.ones[:, :], rhs=C.sq[b][:, :TP],
                                                    start=(kc == 0), stop=(kc == kcn - 1)),
             reads=[C.sq[b], C.ones], writes=[ps])
    P.op("act", lambda e: e.activation(out=C.rstd[:, :TP], in_=ps[:, :TP], func=AF.Sqrt,
                                       scale=1.0 / (kcn * 128), bias=C.epsT[:, 0:1]),
         reads=[ps, C.epsT], writes=[C.rstd])
    P.op("dve", lambda e: e.reciprocal(out=C.rstd[:, :TP], in_=C.rstd[:, :TP]), reads=[C.rstd], writes=[C.rstd])
    for kc in range(kcn):
        P.op("dve", lambda e, kc=kc: e.scalar_tensor_tensor(
            out=C.xnc(kc, TP), in0=C.h32(kc, TP), scalar=g_tile[:, kc:kc + 1], op0=ALU.mult,
            in1=C.rstd[:, :TP], op1=ALU.mult),
            reads=[C.R, g_tile, C.rstd], writes=[C.xn])


def emit_gemm1(P, C, Wd, TP):
    for j in range(JC):
        b = j % 2
        wbuf = C.wb[b]
        for s in range(2):
            P.dma("pool", wbuf, wbuf[:, s * 4096:(s + 1) * 4096], Wd,
                  Wd.ap()[s * JC + j].rearrange("p k m -> p (k m)"), max_dma_last_dim=8192)
        for s in range(2):
            pb = C.ps[2 * b + s]
            for kc in range(KC):
                P.op("pe", lambda e, s=s, kc=kc, pb=pb, wbuf=wbuf: e.matmul(
                    pb[:, :TP], lhsT=wbuf[:, s * 4096 + kc * 128: s * 4096 + (kc + 1) * 128],
                    rhs=C.xnc(kc, TP), start=(kc == 0), stop=(kc == KC - 1)),
                    reads=[wbuf, C.xn], writes=[pb])
        P.op("act", lambda e, b=b: e.activation(out=C.sa[b][:, :TP], in_=C.ps[2 * b][:, :TP], func=AF.Silu),
             reads=[C.ps[2 * b]], writes=[C.sa[b]])
        P.op("dve", lambda e, b=b, j=j: e.tensor_tensor(out=C.g(j, TP), in0=C.sa[b][:, :TP],
                                                         in1=C.ps[2 * b + 1][:, :TP], op=ALU.mult),
             reads=[C.sa[b], C.ps[2 * b + 1]], writes=[C.R])


def emit_linear(P, C, Wd, mcs, kcn, rhs_fn, rhs_bufs, TP, epilogue):
    for i, m in enumerate(mcs):
        b = C.wcnt % 2
        C.wcnt += 1
        wbuf = C.wb[b]
        n = kcn * 128
        half = (kcn // 2) * 128
        src = Wd.ap()[m].rearrange("p k m -> p (k m)")
        P.dma("pool", wbuf, wbuf[:, 0:half], Wd, src[:, 0:half], max_dma_last_dim=8192)
        P.dma("pool", wbuf, wbuf[:, half:n], Wd, src[:, half:n], max_dma_last_dim=8192)
        pb = C.ps[4 + b]
        for kc in range(kcn):
            P.op("pe", lambda e, kc=kc, pb=pb, wbuf=wbuf: e.matmul(
                pb[:, :TP], lhsT=wbuf[:, kc * 128:(kc + 1) * 128], rhs=rhs_fn(kc),
                start=(kc == 0), stop=(kc == kcn - 1)),
                reads=[wbuf] + list(rhs_bufs), writes=[pb])
        epilogue(i, m, pb)


def epi_residual(P, C, h_in, h_out, tok0, TP, scale):
    def epi(i, m, pb):
        b = C.ecnt % 2
        C.ecnt += 1
        P.dma("sp", C.hin[b], C.hin[b][:, :TP], h_in, h_in.ap()[m * 128:(m + 1) * 128, tok0:tok0 + TP])
        P.op("dve", lambda e: e.scalar_tensor_tensor(
            out=C.hout[b][:, :TP], in0=pb[:, :TP], scalar=float(scale), op0=ALU.mult,
            in1=C.hin[b][:, :TP], op1=ALU.add),
            reads=[pb, C.hin[b]], writes=[C.hout[b]])
        P.dma("sp", h_out, h_out.ap()[m * 128:(m + 1) * 128, tok0:tok0 + TP], C.hout[b], C.hout[b][:, :TP])
    return epi


def epi_store(P, C, out_d, tok0, TP, row0=0, eng="act"):
    def epi(i, m, pb):
        b = C.ecnt % 2
        C.ecnt += 1
        if eng == "act":
            P.op("act", lambda e: e.activation(out=C.hout[b][:, :TP], in_=pb[:, :TP], func=AF.Copy),
                 reads=[pb], writes=[C.hout[b]])
        else:
            P.op("dve", lambda e: e.tensor_copy(out=C.hout[b][:, :TP], in_=pb[:, :TP]),
                 reads=[pb], writes=[C.hout[b]])
        r = row0 + i * 128
        P.dma("sp", out_d, out_d.ap()[r:r + 128, tok0:tok0 + TP], C.hout[b], C.hout[b][:, :TP])
    return epi


def emit_ffn(P, C, h_in, h_out, g_tile, Wi, Wo, tok0, TP):
    emit_load_h(P, C, h_in, tok0, TP)
    emit_norm(P, C, g_tile, TP)
    emit_gemm1(P, C, Wi, TP)
    emit_linear(P, C, Wo, list(range(KC)), JC, lambda j: C.g(j, TP), [C.R], TP,
                epi_residual(P, C, h_in, h_out, tok0, TP, 0.5))


def build_stage_a(TC):
    TP = min(512, TC)
    nc = bass.Bass("TRN2", target_bir_lowering=False)
    P = Prog(nc)
    xT = P.dram("xT", [D, TC], F32, kind="ExternalInput")
    Wi = P.dram("w_in", [2 * JC, 128, KC, 128], F32, kind="ExternalInput")
    Wo = P.dram("w_out", [KC, 128, JC, 128], F32, kind="ExternalInput")
    Ws = P.dram("ssm_w_in", [SSM_COLS // 128, 128, KC, 128], F32, kind="ExternalInput")
    g1d = P.dram("g_ffn", [128, KC], F32, kind="ExternalInput")
    g2d = P.dram("g_mix", [128, KC], F32, kind="ExternalInput")
    h1 = P.dram("h1T", [D, TC], F32, kind="ExternalOutput")
    zx = P.dram("zxT", [SSM_COLS, TC], F32, kind="ExternalOutput")
    C = Ctx(P, TP)
    g1 = load_vec(P, "g1", g1d, KC)
    g2 = load_vec(P, "g2", g2d, KC)
    for p in range(TC // TP):
        tok0 = p * TP
        emit_ffn(P, C, xT, h1, g1, Wi, Wo, tok0, TP)
        emit_load_h(P, C, h1, tok0, TP)
        emit_norm(P, C, g2, TP)
        emit_linear(P, C, Ws, list(range(SSM_COLS // 128)), KC, lambda kc: C.xnc(kc, TP), [C.xn], TP,
                    epi_store(P, C, zx, tok0, TP))
    P.wait_all("sp", [h1, zx])
    P.emit()
    return nc


def core_blocks(S):
    nb = S // 128
    per = nb // NCORE
    out = []
    for c in range(NCORE):
        bl = []
        for i in range(per):
            base = (i // 2) * 16
            bl.append(base + (c if i % 2 == 0 else 15 - c))
        out.append(bl)
    return out


def token_index(S):
    cb = core_blocks(S)
    return [np.concatenate([np.arange(b * 128, (b + 1) * 128) for b in bl]) for bl in cb]


def build_stage_b(S):
    nc = bass.Bass("TRN2", target_bir_lowering=False)
    P = Prog(nc)
    ei = "ExternalInput"
    zT = P.dram("zT", [1024, S], F32, kind=ei)
    xT = P.dram("xT", [1024, S], F32, kind=ei)
    BT = P.dram("BT", [128, S], F32, kind=ei)
    CT = P.dram("CT", [128, S], F32, kind=ei)
    dtT = P.dram("dtT", [16, S], F32, kind=ei)
    cstd = P.dram("consts", [128, 384], F32, kind=ei)
    cwxd = P.dram("cwx", [128, 32], F32, kind=ei)
    cbxd = P.dram("cbx", [128, 8], F32, kind=ei)
    cwbcd = P.dram("cwbc", [128, 8], F32, kind=ei)
    cbbcd = P.dram("cbbc", [128, 2], F32, kind=ei)
    dtbd = P.dram("dtb", [16, 1], F32, kind=ei)
    alogd = P.dram("alog_bc", [128, 16], F32, kind=ei)
    dcold = P.dram("dcol", [128, 8], F32, kind=ei)
    ngd = P.dram("ngcol", [128, 8], F32, kind=ei)
    yT = P.dram("yT", [1024, S], F32, kind="ExternalOutput")

    def ld(name, d, shape):
        t = P.sbuf(shape, F32, name, dma=True)
        P.dma("sp", t, t[:, :], d, d.ap())
        return t
    cst = ld("cst", cstd, [128, 384])
    cwx = ld("cwx_s", cwxd, [128, 32])
    cbx = ld("cbx_s", cbxd, [128, 8])
    cwbc = ld("cwbc_s", cwbcd, [128, 8])
    cbbc = ld("cbbc_s", cbbcd, [128, 2])
    dtb = ld("dtb_s", dtbd, [16, 1])
    A_bc = ld("A_bc", alogd, [128, 16])
    dcol = ld("dcol_s", dcold, [128, 8])
    ngc = ld("ngc_s", ngd, [128, 8])
    U = cst[:, 0:128]
    G = cst[:, 128:256]
    I32f = cst[:, 256:384]
    identb = P.sbuf([128, 128], BF16, "identb")
    ones = P.sbuf([128, 128], F32, "ones")
    epsT = P.sbuf([128, 1], F32, "epsT")
    P.op("dve", lambda e: e.tensor_copy(out=identb[:, :], in_=I32f), reads=[cst], writes=[identb])
    P.op("dve", lambda e: e.memset(ones[:, :], 1.0), writes=[ones])
    P.op("dve", lambda e: e.memset(epsT[:, :], EPS), writes=[epsT])
    P.op("act", lambda e: e.activation(out=A_bc[:, :], in_=A_bc[:, :], func=AF.Exp), reads=[A_bc], writes=[A_bc])
    P.op("dve", lambda e: e.tensor_scalar(out=A_bc[:, :], in0=A_bc[:, :], scalar1=-1.0, scalar2=None, op0=ALU.mult),
         reads=[A_bc], writes=[A_bc])

    SC = 512
    xraw = [P.sbuf([128, 8, SC + 3], F32, f"xraw{i}", dma=True) for i in range(2)]
    zraw = [P.sbuf([128, 8, SC], F32, f"zraw{i}", dma=True) for i in range(2)]
    bcraw = [P.sbuf([128, 2, SC + 3], F32, f"bcraw{i}", dma=True) for i in range(2)]
    dtraw = [P.sbuf([16, SC], F32, f"dtraw{i}", dma=True) for i in range(2)]
    xs = [P.sbuf([128, 8, SC], F32, f"xs{i}") for i in range(2)]
    xsb = P.sbuf([128, 8, SC], BF16, "xsb")
    bcc = P.sbuf([128, 2, SC], F32, "bcc")
    BCb = P.sbuf([128, 2, SC], BF16, "BCb")
    e1 = P.sbuf([16, SC], F32, "e1")
    dtf = P.sbuf([16, SC], F32, "dtf")
    dt_tok = P.sbuf([128, 16], F32, "dt_tok")
    a_tok = P.sbuf([128, 16], F32, "a_tok")
    E = P.sbuf([128, 48], F32, "Estat")
    lhsD = [P.sbuf([128, 4, 128], F32, f"lhsD{i}") for i in range(2)]
    expD = [P.sbuf([128, 4, 128], F32, f"expD{i}") for i in range(2)]
    MT = [P.sbuf([128, 4, 128], BF16, f"MT{i}") for i in range(2)]
    CBm = P.sbuf([128, 128], F32, "CBm")
    Btok = P.sbuf([128, 128], BF16, "Btok")
    xdt = P.sbuf([128, 1024], BF16, "xdt")
    xw = P.sbuf([128, 1024], BF16, "xw")
    tmpy = [P.sbuf([128, 512], F32, f"tmpy{i}") for i in range(2)]
    S32 = P.sbuf([128, 1024], F32, "S32")
    Sb = P.sbuf([128, 1024], BF16, "Sb")
    stmp = P.sbuf([128, 512], F32, "stmp")
    ytok = P.sbuf([128, 4, 1024], F32, "ytok")
    y3 = P.sbuf([128, 8, SC], F32, "y3")
    sqp = [P.sbuf([128, SC], F32, f"sqp{i}") for i in range(2)]
    rstd = P.sbuf([128, SC], F32, "rstd")
    yo = [P.sbuf([128, SC], F32, f"yo{i}") for i in range(2)]
    ps = [P.psum([128, 512], F32, f"psb{i}") for i in range(8)]
    b0b = ps[0].h.ap().bitcast(BF16)
    b1 = ps[1]
    b1b = ps[1].h.ap().bitcast(BF16)
    P.op("dve", lambda e: e.memset(S32[:, :], 0.0), writes=[S32])
    P.op("dve", lambda e: e.memset(Sb[:, :], 0.0), writes=[Sb])

    xTv = xT.ap().rearrange("(cc p) t -> p cc t", p=128)
    zTv = zT.ap().rearrange("(cc p) t -> p cc t", p=128)

    for sc in range(S // SC):
        b = sc % 2
        tok0 = sc * SC
        if sc == 0:
            P.op("dve", lambda e, b=b: e.memset(xraw[b][:, :, 0:3], 0.0), writes=[xraw[b]])
            P.op("dve", lambda e, b=b: e.memset(bcraw[b][:, :, 0:3], 0.0), writes=[bcraw[b]])
            P.dma("sp", xraw[b], xraw[b][:, :, 3:SC + 3], xT, xTv[:, :, 0:SC])
            P.dma("sp", bcraw[b], bcraw[b][:, 0, 3:SC + 3], BT, BT.ap()[:, 0:SC])
            P.dma("sp", bcraw[b], bcraw[b][:, 1, 3:SC + 3], CT, CT.ap()[:, 0:SC])
        else:
            P.dma("sp", xraw[b], xraw[b][:, :, :], xT, xTv[:, :, tok0 - 3:tok0 + SC])
            P.dma("sp", bcraw[b], bcraw[b][:, 0, :], BT, BT.ap()[:, tok0 - 3:tok0 + SC])
            P.dma("sp", bcraw[b], bcraw[b][:, 1, :], CT, CT.ap()[:, tok0 - 3:tok0 + SC])
        P.dma("sp", zraw[b], zraw[b][:, :, :], zT, zTv[:, :, tok0:tok0 + SC])
        P.dma("sp", dtraw[b], dtraw[b][:, :], dtT, dtT.ap()[:, tok0:tok0 + SC])

        for cc in range(8):
            P.op("dve", lambda e, cc=cc, b=b: e.tensor_scalar(
                out=xs[b][:, cc, :], in0=xraw[b][:, cc, 0:SC], scalar1=cwx[:, cc * 4:cc * 4 + 1],
                scalar2=cbx[:, cc:cc + 1], op0=ALU.mult, op1=ALU.add),
                reads=[xraw[b], cwx, cbx], writes=[xs[b]])
            for k in range(1, 4):
                P.op("dve", lambda e, cc=cc, b=b, k=k: e.scalar_tensor_tensor(
                    out=xs[b][:, cc, :], in0=xraw[b][:, cc, k:k + SC], scalar=cwx[:, cc * 4 + k:cc * 4 + k + 1],
                    op0=ALU.mult, in1=xs[b][:, cc, :], op1=ALU.add),
                    reads=[xraw[b], cwx, xs[b]], writes=[xs[b]])
        for i in range(2):
            P.op("dve", lambda e, i=i, b=b: e.tensor_scalar(
                out=bcc[:, i, :], in0=bcraw[b][:, i, 0:SC], scalar1=cwbc[:, i * 4:i * 4 + 1],
                scalar2=cbbc[:, i:i + 1], op0=ALU.mult, op1=ALU.add),
                reads=[bcraw[b], cwbc, cbbc], writes=[bcc])
            for k in range(1, 4):
                P.op("dve", lambda e, i=i, b=b, k=k: e.scalar_tensor_tensor(
                    out=bcc[:, i, :], in0=bcraw[b][:, i, k:k + SC], scalar=cwbc[:, i * 4 + k:i * 4 + k + 1],
                    op0=ALU.mult, in1=bcc[:, i, :], op1=ALU.add),
                    reads=[bcraw[b], cwbc, bcc], writes=[bcc])
        P.op("act", lambda e, b=b: e.activation(out=xsb[:, :, :], in_=xs[b][:, :, :], func=AF.Silu), reads=[xs[b]], writes=[xsb])
        P.op("act", lambda e, b=b: e.activation(out=xs[b][:, :, :], in_=xs[b][:, :, :], func=AF.Silu), reads=[xs[b]], writes=[xs[b]])
        P.op("act", lambda e: e.activation(out=BCb[:, :, :], in_=bcc[:, :, :], func=AF.Silu), reads=[bcc], writes=[BCb])
        P.op("act", lambda e, b=b: e.activation(out=zraw[b][:, :, :], in_=zraw[b][:, :, :], func=AF.Silu),
             reads=[zraw[b]], writes=[zraw[b]])
        P.op("act", lambda e, b=b: e.activation(out=e1[:, :], in_=dtraw[b][:, :], func=AF.Exp, bias=dtb[:, 0:1]),
             reads=[dtraw[b], dtb], writes=[e1])
        P.op("act", lambda e: e.activation(out=dtf[:, :], in_=e1[:, :], func=AF.Ln, bias=ones[0:16, 0:1]),
             reads=[e1, ones], writes=[dtf])

        for ci in range(4):
            c0 = ci * 128
            P.op("pe", lambda e, c0=c0: e.transpose(out=b1[:, 0:16], in_=dtf[:, c0:c0 + 128], identity=I32f[0:16, 0:16]),
                 reads=[dtf, cst], writes=[b1])
            P.op("dve", lambda e: e.tensor_copy(out=dt_tok[:, :], in_=b1[:, 0:16]), reads=[b1], writes=[dt_tok])
            P.op("dve", lambda e: e.tensor_tensor(out=a_tok[:, :], in0=b1[:, 0:16], in1=A_bc[:, :], op=ALU.mult),
                 reads=[b1, A_bc], writes=[a_tok])
            for q, M_ in enumerate([U, G, ones[:, :]]):
                P.op("pe", lambda e, q=q, M_=M_: e.matmul(b1[:, 32 + 16 * q:48 + 16 * q], lhsT=M_, rhs=a_tok[:, :],
                                                          start=True, stop=True),
                     reads=[cst, ones, a_tok], writes=[b1])
            P.op("act", lambda e: e.activation(out=E[:, :], in_=b1[:, 32:80], func=AF.Exp), reads=[b1], writes=[E])
            for cc in range(8):
                P.op("pe", lambda e, cc=cc, c0=c0: e.transpose(out=b0b[:, cc * 128:(cc + 1) * 128],
                                                               in_=xsb[:, cc, c0:c0 + 128], identity=identb[:, :]),
                     reads=[xsb, identb], writes=[ps[0]])
            P.op("dve", lambda e: e.tensor_tensor(
                out=xdt[:, :].rearrange("p (r d) -> p r d", d=64), in0=b0b.rearrange("p (r d) -> p r d", d=64),
                in1=dt_tok[:, :].unsqueeze(2).to_broadcast([128, 16, 64]), op=ALU.mult),
                reads=[ps[0], dt_tok], writes=[xdt])
            P.op("dve", lambda e: e.tensor_tensor(
                out=xw[:, :].rearrange("p (r d) -> p r d", d=64), in0=xdt[:, :].rearrange("p (r d) -> p r d", d=64),
                in1=E[:, 16:32].unsqueeze(2).to_broadcast([128, 16, 64]), op=ALU.mult),
                reads=[xdt, E], writes=[xw])
            P.op("pe", lambda e, c0=c0: e.transpose(out=b1b[:, 256:384], in_=BCb[:, 0, c0:c0 + 128], identity=identb[:, :]),
                 reads=[BCb, identb], writes=[b1])
            P.op("act", lambda e: e.activation(out=Btok[:, :], in_=b1b[:, 256:384], func=AF.Copy), reads=[b1], writes=[Btok])
            P.op("pe", lambda e, c0=c0: e.matmul(b1[:, 256:384], lhsT=BCb[:, 0, c0:c0 + 128], rhs=BCb[:, 1, c0:c0 + 128],
                                                 start=True, stop=True),
                 reads=[BCb], writes=[b1])
            P.op("dve", lambda e: e.tensor_tensor(out=CBm[:, :], in0=b1[:, 256:384], in1=U, op=ALU.mult),
                 reads=[b1, cst], writes=[CBm])
            for h in range(2):
                hs = slice(h * 512, (h + 1) * 512)
                P.op("pe", lambda e, c0=c0, hs=hs: e.matmul(ps[4][:, :], lhsT=BCb[:, 1, c0:c0 + 128], rhs=Sb[:, hs],
                                                            start=True, stop=True),
                     reads=[BCb, Sb], writes=[ps[4]])
                for hg in (2 * h, 2 * h + 1):
                    db = hg % 2
                    P.op("dve", lambda e, hg=hg, db=db: e.tensor_tensor(
                        out=lhsD[db][:, :, :], in0=G.unsqueeze(1).to_broadcast([128, 4, 128]),
                        in1=a_tok[:, hg * 4:hg * 4 + 4].unsqueeze(2).to_broadcast([128, 4, 128]), op=ALU.mult),
                        reads=[cst, a_tok], writes=[lhsD[db]])
                    for q in range(4):
                        P.op("pe", lambda e, q=q, db=db: e.matmul(ps[2 + db][:, q * 128:(q + 1) * 128], lhsT=lhsD[db][:, q, :],
                                                                  rhs=U, start=True, stop=True),
                             reads=[lhsD[db], cst], writes=[ps[2 + db]])
                    P.op("act", lambda e, db=db: e.activation(out=expD[db][:, :, :].rearrange("p a b -> p (a b)"),
                                                              in_=ps[2 + db][:, :], func=AF.Exp),
                         reads=[ps[2 + db]], writes=[expD[db]])
                    P.op("dve", lambda e, db=db: e.tensor_tensor(
                        out=MT[db][:, :, :], in0=expD[db][:, :, :],
                        in1=CBm[:, :].unsqueeze(1).to_broadcast([128, 4, 128]), op=ALU.mult),
                        reads=[expD[db], CBm], writes=[MT[db]])
                    for q in range(4):
                        r = hg * 4 + q
                        rl = r % 8
                        P.op("pe", lambda e, q=q, db=db, r=r, rl=rl: e.matmul(
                            ps[5][:, rl * 64:(rl + 1) * 64], lhsT=MT[db][:, q, :], rhs=xdt[:, r * 64:(r + 1) * 64],
                            start=True, stop=True),
                            reads=[MT[db], xdt], writes=[ps[5]])
                P.op("dve", lambda e, h=h: e.tensor_tensor(
                    out=tmpy[h][:, :].rearrange("p (r d) -> p r d", d=64), in0=ps[4][:, :].rearrange("p (r d) -> p r d", d=64),
                    in1=E[:, h * 8:h * 8 + 8].unsqueeze(2).to_broadcast([128, 8, 64]), op=ALU.mult),
                    reads=[ps[4], E], writes=[tmpy[h]])
                P.op("dve", lambda e, h=h, ci=ci, hs=hs: e.tensor_tensor(out=ytok[:, ci, hs], in0=tmpy[h][:, :], in1=ps[5][:, :],
                                                                       op=ALU.add),
                     reads=[tmpy[h], ps[5]], writes=[ytok])
                P.op("pe", lambda e, hs=hs: e.matmul(ps[6][:, :], lhsT=Btok[:, :], rhs=xw[:, hs], start=True, stop=True),
                     reads=[Btok, xw], writes=[ps[6]])
                P.op("dve", lambda e, h=h, hs=hs: e.tensor_tensor(
                    out=stmp[:, :].rearrange("p (r d) -> p r d", d=64), in0=S32[:, hs].rearrange("p (r d) -> p r d", d=64),
                    in1=E[:, 32 + h * 8:32 + h * 8 + 8].unsqueeze(2).to_broadcast([128, 8, 64]), op=ALU.mult),
                    reads=[S32, E], writes=[stmp])
                P.op("dve", lambda e, hs=hs: e.tensor_tensor(out=S32[:, hs], in0=stmp[:, :], in1=ps[6][:, :], op=ALU.add),
                     reads=[stmp, ps[6]], writes=[S32])
                P.op("act", lambda e, hs=hs: e.activation(out=Sb[:, hs], in_=S32[:, hs], func=AF.Copy), reads=[S32], writes=[Sb])

        for cc in range(8):
            for ci in range(4):
                P.op("pe", lambda e, cc=cc, ci=ci: e.transpose(out=ps[7][:, ci * 128:(ci + 1) * 128],
                                                               in_=ytok[:, ci, cc * 128:(cc + 1) * 128], identity=I32f),
                     reads=[ytok, cst], writes=[ps[7]])
            P.op("dve", lambda e, cc=cc, b=b: e.scalar_tensor_tensor(
                out=y3[:, cc, :], in0=xs[b][:, cc, :], scalar=dcol[:, cc:cc + 1], op0=ALU.mult, in1=ps[7][:, :], op1=ALU.add),
                reads=[xs[b], dcol, ps[7]], writes=[y3])
            P.op("dve", lambda e, cc=cc, b=b: e.tensor_tensor(out=y3[:, cc, :], in0=y3[:, cc, :], in1=zraw[b][:, cc, :], op=ALU.mult),
                 reads=[y3, zraw[b]], writes=[y3])
            P.op("act", lambda e, cc=cc: e.activation(out=sqp[cc % 2][:, :], in_=y3[:, cc, :], func=AF.Square),
                 reads=[y3], writes=[sqp[cc % 2]])
            P.op("pe", lambda e, cc=cc: e.matmul(ps[6][:, :], lhsT=ones[:, :], rhs=sqp[cc % 2][:, :], start=(cc == 0), stop=(cc == 7)),
                 reads=[ones, sqp[cc % 2]], writes=[ps[6]])
        P.op("act", lambda e: e.activation(out=rstd[:, :], in_=ps[6][:, :], func=AF.Sqrt, scale=1.0 / 1024, bias=epsT[:, 0:1]),
             reads=[ps[6], epsT], writes=[rstd])
        P.op("dve", lambda e: e.reciprocal(out=rstd[:, :], in_=rstd[:, :]), reads=[rstd], writes=[rstd])
        for cc in range(8):
            P.op("dve", lambda e, cc=cc: e.scalar_tensor_tensor(
                out=yo[cc % 2][:, :], in0=y3[:, cc, :], scalar=ngc[:, cc:cc + 1], op0=ALU.mult, in1=rstd[:, :], op1=ALU.mult),
                reads=[y3, ngc, rstd], writes=[yo[cc % 2]])
            P.dma("sp", yT, yT.ap()[cc * 128:(cc + 1) * 128, tok0:tok0 + SC], yo[cc % 2], yo[cc % 2][:, :])
    P.wait_all("sp", [yT])
    P.emit()
    return nc


def ssd_consts():
    j = np.arange(128)
    U = (j[:, None] <= j[None, :]).astype(np.float32)
    G = (j[:, None] > j[None, :]).astype(np.float32)
    I = np.eye(128, dtype=np.float32)
    return np.ascontiguousarray(np.concatenate([U, G, I], axis=1))


def stage_b_inputs(zx_full, g, ssm_conv_w, ssm_conv_b, ssm_dt_bias, ssm_a_log, ssm_d, ssm_norm):
    DI = 8192
    x0 = DI + g * 1024
    b0 = 2 * DI + g * 128
    c0 = 2 * DI + 1024 + g * 128
    d0 = 2 * DI + 2048 + g * 16
    cw = ssm_conv_w
    cb = ssm_conv_b
    xw = cw[:, g * 1024:(g + 1) * 1024]
    cwx = xw.reshape(4, 8, 128).transpose(2, 1, 0).reshape(128, 32)
    cbx = cb[g * 1024:(g + 1) * 1024].reshape(8, 128).T
    wB = cw[:, DI + g * 128:DI + (g + 1) * 128].T
    wC = cw[:, DI + 1024 + g * 128:DI + 1024 + (g + 1) * 128].T
    bB = cb[DI + g * 128:DI + (g + 1) * 128]
    bC = cb[DI + 1024 + g * 128:DI + 1024 + (g + 1) * 128]
    hd = slice(g * 16, (g + 1) * 16)
    dcol = np.repeat(ssm_d[hd], 64).reshape(8, 128).T
    ng = ssm_norm[g * 1024:(g + 1) * 1024].reshape(8, 128).T
    f = np.ascontiguousarray
    return {
        "zT": f(zx_full[:, g * 1024:(g + 1) * 1024].T), "xT": f(zx_full[:, x0:x0 + 1024].T),
        "BT": f(zx_full[:, b0:b0 + 128].T), "CT": f(zx_full[:, c0:c0 + 128].T), "dtT": f(zx_full[:, d0:d0 + 16].T),
        "consts": ssd_consts(), "cwx": f(cwx), "cbx": f(cbx), "cwbc": f(np.concatenate([wB, wC], axis=1)),
        "cbbc": f(np.stack([bB, bC], axis=1)), "dtb": f(ssm_dt_bias[hd].reshape(16, 1)),
        "alog_bc": f(np.broadcast_to(ssm_a_log[hd][None, :], (128, 16))), "dcol": f(dcol), "ngcol": f(ng),
    }


ROPE_THETA = 500000.0


def rope_tables(pos):
    half = 16
    inv = (np.float32(ROPE_THETA) ** (-np.arange(half, dtype=np.float32) / np.float32(half))).astype(np.float32)
    ang = pos.astype(np.float32)[None, :] * inv[:, None]
    c = np.cos(ang).astype(np.float32)
    s = np.sin(ang).astype(np.float32)
    return np.ascontiguousarray(np.concatenate([c, c], 0)), np.ascontiguousarray(np.concatenate([s, s], 0))


def rope_perm():
    Pm = np.zeros((32, 32), np.float32)
    for i in range(16):
        Pm[i + 16, i] = -1.0
        Pm[i, i + 16] = 1.0
    return Pm


class RopeCtx:
    def __init__(self, P, C, cd, sd, pmd, TC):
        self.cosT = P.sbuf([32, TC], F32, "ropeC_s", dma=True)
        self.sinT = P.sbuf([32, TC], F32, "ropeS_s", dma=True)
        self.Pm = P.sbuf([32, 32], F32, "ropeP_s", dma=True)
        P.dma("sp", self.cosT, self.cosT[:, :], cd, cd.ap())
        P.dma("sp", self.sinT, self.sinT[:, :], sd, sd.ap())
        P.dma("sp", self.Pm, self.Pm[:, :], pmd, pmd.ap())
        self.t1 = P.sbuf([32, 512], F32, "ropet1")
        self.t2 = P.sbuf([32, 512], F32, "ropet2")
        self.rk = P.sbuf([128, 512], F32, "rk")


def emit_headnorm_rope(P, C, RC, pb, gcol_ap, gbuf, dst_buf, dst_ap, tok0, TP, rope=True):
    b = C.ecnt % 2
    ps7 = C.ps[7]
    P.op("act", lambda e: e.activation(out=C.sq[b][:, :TP], in_=pb[:, :TP], func=AF.Square), reads=[pb], writes=[C.sq[b]])
    P.op("pe", lambda e: e.matmul(ps7[:, :TP], lhsT=C.ones[:, :], rhs=C.sq[b][:, :TP], start=True, stop=True),
         reads=[C.ones, C.sq[b]], writes=[ps7])
    P.op("act", lambda e: e.activation(out=RC.rk[:, :TP], in_=ps7[:, :TP], func=AF.Sqrt, scale=1.0 / 128, bias=C.epsT[:, 0:1]),
         reads=[ps7, C.epsT], writes=[RC.rk])
    P.op("dve", lambda e: e.reciprocal(out=RC.rk[:, :TP], in_=RC.rk[:, :TP]), reads=[RC.rk], writes=[RC.rk])
    P.op("dve", lambda e: e.scalar_tensor_tensor(out=dst_ap, in0=pb[:, :TP], scalar=gcol_ap, op0=ALU.mult,
                                                 in1=RC.rk[:, :TP], op1=ALU.mult),
         reads=[pb, gbuf, RC.rk], writes=[dst_buf])
    if rope:
        P.op("pe", lambda e: e.matmul(ps7[0:32, :TP], lhsT=RC.Pm[:, :], rhs=dst_ap[0:32, :], start=True, stop=True),
             reads=[RC.Pm, dst_buf], writes=[ps7])
        P.op("dve", lambda e: e.tensor_tensor(out=RC.t1[:, :TP], in0=dst_ap[0:32, :], in1=RC.cosT[:, tok0:tok0 + TP], op=ALU.mult),
             reads=[dst_buf, RC.cosT], writes=[RC.t1])
        P.op("dve", lambda e: e.tensor_tensor(out=RC.t2[:, :TP], in0=ps7[0:32, :TP], in1=RC.sinT[:, tok0:tok0 + TP], op=ALU.mult),
             reads=[ps7, RC.sinT], writes=[RC.t2])
        P.op("dve", lambda e: e.tensor_tensor(out=dst_ap[0:32, :], in0=RC.t1[:, :TP], in1=RC.t2[:, :TP], op=ALU.add),
             reads=[RC.t1, RC.t2], writes=[dst_buf])


def build_stage_c(TC, with_ffn=True):
    TP = min(512, TC)
    nc = bass.Bass("TRN2", target_bir_lowering=False)
    P = Prog(nc)
    ei = "ExternalInput"
    yT = P.dram("yT", [8192, TC], F32, kind=ei)
    h1 = P.dram("h1T", [D, TC], F32, kind=ei)
    Wso = P.dram("w_so", [KC, 128, 64, 128], F32, kind=ei)
    Wkv = P.dram("w_kv", [24, 128, KC, 128], F32, kind=ei)
    gkvd = P.dram("g_kv", [128, KC], F32, kind=ei)
    gkd = P.dram("g_k", [128, 2], F32, kind=ei)
    rcd = P.dram("ropeC", [32, TC], F32, kind=ei)
    rsd = P.dram("ropeS", [32, TC], F32, kind=ei)
    rpd = P.dram("ropeP", [32, 32], F32, kind=ei)
    if with_ffn:
        Wi = P.dram("w_in", [2 * JC, 128, KC, 128], F32, kind=ei)
        Wo = P.dram("w_out", [KC, 128, JC, 128], F32, kind=ei)
        gfd = P.dram("g_ffn", [128, KC], F32, kind=ei)
        h2 = P.dram("h2T", [D, TC], F32, kind="Internal")
        h3 = P.dram("h3T", [D, TC], F32, kind="ExternalOutput")
    else:
        h2 = P.dram("h3T", [D, TC], F32, kind="ExternalOutput")
        h3 = h2
    kv = P.dram("kvT", [3072, TC], BF16, kind="ExternalOutput")
    C = Ctx(P, TP)
    RC = RopeCtx(P, C, rcd, rsd, rpd, TC)
    hob = [P.sbuf([128, 512], BF16, f"hob{i}") for i in range(2)]
    gkv = load_vec(P, "gkv", gkvd, KC)
    gk = load_vec(P, "gk", gkd, 2)
    if with_ffn:
        gf = load_vec(P, "gf", gfd, KC)
    yv = yT.ap().rearrange("(kc p) t -> p kc t", p=128)
    for p in range(TC // TP):
        tok0 = p * TP
        dst = C.R[:, 0:64 * TP].rearrange("p (kc t) -> p kc t", t=TP)
        for k0 in range(0, 64, 16):
            P.dma("pool", C.R, dst[:, k0:k0 + 16, :], yT, yv[:, k0:k0 + 16, tok0:tok0 + TP])
        emit_linear(P, C, Wso, list(range(KC)), 64, lambda kc: C.g(kc, TP), [C.R], TP,
                    epi_residual(P, C, h1, h2, tok0, TP, 1.0))
        if with_ffn:
            emit_ffn(P, C, h2, h3, gf, Wi, Wo, tok0, TP)
        emit_load_h(P, C, h3, tok0, TP)
        emit_norm(P, C, gkv, TP)

        def epi_kv(i, m, pb, tok0=tok0):
            br = m // 4
            b = C.ecnt % 2
            if br in (2, 4):
                gi = 0 if br == 2 else 1
                emit_headnorm_rope(P, C, RC, pb, gk[:, gi:gi + 1], gk, C.hout[b], C.hout[b][:, :TP], tok0, TP)
                P.op("act", lambda e: e.activation(out=hob[b][:, :TP], in_=C.hout[b][:, :TP], func=AF.Copy),
                     reads=[C.hout[b]], writes=[hob[b]])
            else:
                P.op("act", lambda e: e.activation(out=hob[b][:, :TP], in_=pb[:, :TP], func=AF.Copy),
                     reads=[pb], writes=[hob[b]])
            C.ecnt += 1
            P.dma("sp", kv, kv.ap()[m * 128:(m + 1) * 128, tok0:tok0 + TP], hob[b], hob[b][:, :TP])
        emit_linear(P, C, Wkv, list(range(24)), KC, lambda kc: C.xnc(kc, TP), [C.xn], TP, epi_kv)
    P.wait_all("sp", [h3, kv])
    P.emit()
    return nc


def build_stage_d1(TC, with_ffn=True):
    TP = min(512, TC)
    nc = bass.Bass("TRN2", target_bir_lowering=False)
    P = Prog(nc)
    ei = "ExternalInput"
    h3 = P.dram("h3T", [D, TC], F32, kind=ei)
    Wq = P.dram("w_q", [33, 128, KC, 128], F32, kind=ei)
    gmd = P.dram("g_mix", [128, KC], F32, kind=ei)
    gqd = P.dram("g_q", [128, 1], F32, kind=ei)
    rcd = P.dram("ropeC", [32, TC], F32, kind=ei)
    rsd = P.dram("ropeS", [32, TC], F32, kind=ei)
    rpd = P.dram("ropeP", [32, 32], F32, kind=ei)
    if with_ffn:
        Wi = P.dram("w_in", [2 * JC, 128, KC, 128], F32, kind=ei)
        Wo = P.dram("w_out", [KC, 128, JC, 128], F32, kind=ei)
        gfd = P.dram("g_ffn", [128, KC], F32, kind=ei)
        h4 = P.dram("h4T", [D, TC], F32, kind="ExternalOutput")
    else:
        h4 = h3
    qT = P.dram("qT", [D, TC], F32, kind="ExternalOutput")
    gT = P.dram("gT", [128, TC], F32, kind="ExternalOutput")
    C = Ctx(P, TP)
    RC = RopeCtx(P, C, rcd, rsd, rpd, TC)
    gm = load_vec(P, "gm", gmd, KC)
    gq = load_vec(P, "gq", gqd, 1)
    if with_ffn:
        gf = load_vec(P, "gf", gfd, KC)
    for p in range(TC // TP):
        tok0 = p * TP
        if with_ffn:
            emit_ffn(P, C, h3, h4, gf, Wi, Wo, tok0, TP)
        emit_load_h(P, C, h4, tok0, TP)
        emit_norm(P, C, gm, TP)

        def epi_q(i, m, pb, tok0=tok0):
            b = C.ecnt % 2
            if m < 32:
                emit_headnorm_rope(P, C, RC, pb, gq[:, 0:1], gq, C.hout[b], C.hout[b][:, :TP], tok0, TP)
                C.ecnt += 1
                P.dma("sp", qT, qT.ap()[m * 128:(m + 1) * 128, tok0:tok0 + TP], C.hout[b], C.hout[b][:, :TP])
            else:
                P.op("act", lambda e: e.activation(out=C.hout[b][:, :TP], in_=pb[:, :TP], func=AF.Sigmoid),
                     reads=[pb], writes=[C.hout[b]])
                C.ecnt += 1
                P.dma("sp", gT, gT.ap()[:, tok0:tok0 + TP], C.hout[b], C.hout[b][:, :TP])
        emit_linear(P, C, Wq, list(range(33)), KC, lambda kc: C.xnc(kc, TP), [C.xn], TP, epi_q)
    outs = [qT, gT] + ([h4] if with_ffn else [])
    P.wait_all("sp", outs)
    P.emit()
    return nc


def build_stage_d3(TC, with_ffn=True):
    TP = min(512, TC)
    nc = bass.Bass("TRN2", target_bir_lowering=False)
    P = Prog(nc)
    ei = "ExternalInput"
    oT = P.dram("oT", [D, TC], F32, kind=ei)
    h4 = P.dram("h4T", [D, TC], F32, kind=ei)
    Wao = P.dram("w_ao", [KC, 128, KC, 128], F32, kind=ei)
    if with_ffn:
        Wi = P.dram("w_in", [2 * JC, 128, KC, 128], F32, kind=ei)
        Wo = P.dram("w_out", [KC, 128, JC, 128], F32, kind=ei)
        gfd = P.dram("g_ffn", [128, KC], F32, kind=ei)
        h5 = P.dram("h5T", [D, TC], F32, kind="Internal")
        out = P.dram("outT", [D, TC], F32, kind="ExternalOutput")
    else:
        h5 = P.dram("outT", [D, TC], F32, kind="ExternalOutput")
        out = h5
    C = Ctx(P, TP)
    if with_ffn:
        gf = load_vec(P, "gf", gfd, KC)
    ov = oT.ap().rearrange("(kc p) t -> p kc t", p=128)
    for p in range(TC // TP):
        tok0 = p * TP
        dst = C.R[:, 0:KC * TP].rearrange("p (kc t) -> p kc t", t=TP)
        for k0 in range(0, KC, 16):
            P.dma("pool", C.R, dst[:, k0:k0 + 16, :], oT, ov[:, k0:k0 + 16, tok0:tok0 + TP])
        emit_linear(P, C, Wao, list(range(KC)), KC, lambda kc: C.g(kc, TP), [C.R], TP,
                    epi_residual(P, C, h4, h5, tok0, TP, 1.0))
        if with_ffn:
            emit_ffn(P, C, h5, out, gf, Wi, Wo, tok0, TP)
    P.wait_all("sp", [out])
    P.emit()
    return nc


SCALE = 128.0 ** -0.5
NEGV = -1.0e30


def build_stage_d2(S, TC, branches=(0, 1, 2), dbg=False):
    NB = TC // 128
    NKTMAX = min(8 * NB, S // 128)
    NCMP = S // 16 - 1
    NCT = (NCMP + 127) // 128
    nc = bass.Bass("TRN2", target_bir_lowering=False)
    P = Prog(nc)
    ei = "ExternalInput"
    qT = P.dram("qT", [D, TC], F32, kind=ei)
    gT = P.dram("gT", [128, TC], F32, kind=ei)
    kcraw = P.dram("kcrawT", [4, 128, S], BF16, kind=ei)
    vcraw = P.dram("vcrawT", [4, 128, S], BF16, kind=ei)
    ksT = P.dram("ksT", [4, 128, S], BF16, kind=ei)
    vsl = P.dram("vs_l", [4, 128, S // 128, 128], BF16, kind=ei)
    kwT = P.dram("kwT", [4, NB, 128, 640], BF16, kind=ei)
    vwl = P.dram("vw_l", [4, NB, 128, 5, 128], BF16, kind=ei)
    w1d = [P.dram(n, [128, 32 * 128], F32, kind=ei) for n in ("w1k", "w1v")]
    w2d = [P.dram(n, [128, 128], F32, kind=ei) for n in ("w2k", "w2v")]
    ped = [P.dram(n, [128, 32], F32, kind=ei) for n in ("pekT", "pevT")]
    gkcd = P.dram("g_kc", [128, 1], F32, kind=ei)
    rccd = P.dram("ropeCc", [32, 512], F32, kind=ei)
    rscd = P.dram("ropeSc", [32, 512], F32, kind=ei)
    rpd = P.dram("ropeP", [32, 32], F32, kind=ei)
    i128d = P.dram("I128", [128, 128], F32, kind=ei)
    mseld = P.dram("Msel", [128, 4 * 128], F32, kind=ei)
    cendd = P.dram("cend_col", [128, 4], F32, kind=ei)
    kposd = P.dram("kpos_col", [128, 64], F32, kind=ei)
    wmd = P.dram("wm", [128, 5 * 128], F32, kind=ei)
    qposd = P.dram("qpos_bc", [128, TC], F32, kind=ei)
    bonusd = P.dram("bonusT", [128, TC], F32, kind=ei)
    validd = P.dram("validT", [128, TC], F32, kind=ei)
    kvald = P.dram("kvalid_col", [128, NB * 5], F32, kind=ei)
    oT = P.dram("oT", [D, TC], F32, kind="ExternalOutput")

    def ld(name, d, shape, dt=F32, eng="sp"):
        t = P.sbuf(shape, dt, name, dma=True)
        P.dma(eng, t, t[:, :], d, d.ap())
        return t
    I128 = ld("I128s", i128d, [128, 128])
    Msel = ld("Msels", mseld, [128, 512])
    cend = ld("cends", cendd, [128, 4])
    kpos = ld("kposs", kposd, [128, 64])
    wm = ld("wms", wmd, [128, 640])
    qpos = ld("qposs", qposd, [128, TC])
    bonus = ld("bonuss", bonusd, [128, TC])
    valid = ld("valids", validd, [128, TC])
    kval = ld("kvals", kvald, [128, NB * 5])
    gkc = ld("gkcs", gkcd, [128, 1])
    rcc = ld("rccs", rccd, [32, 512])
    rsc = ld("rscs", rscd, [32, 512])
    rpm = ld("rpms", rpd, [32, 32])
    identb = P.sbuf([128, 128], BF16, "identb")
    ones = P.sbuf([128, 128], F32, "ones")
    onesb = P.sbuf([128, 128], BF16, "onesb")
    epsT = P.sbuf([128, 1], F32, "epsT")
    P.op("dve", lambda e: e.tensor_copy(out=identb[:, :], in_=I128[:, :]), reads=[I128], writes=[identb])
    P.op("dve", lambda e: e.memset(ones[:, :], 1.0), writes=[ones])
    P.op("dve", lambda e: e.memset(onesb[:, :], 1.0), writes=[onesb])
    P.op("dve", lambda e: e.memset(epsT[:, :], EPS), writes=[epsT])

    kcT = P.sbuf([128, 4, 512], F32, "kcT")
    vctok = P.sbuf([128, 4, 4, 128], F32, "vctok")
    P.op("dve", lambda e: e.memset(kcT[:, :, :], 0.0), writes=[kcT])
    P.op("dve", lambda e: e.memset(vctok[:, :, :, :], 0.0), writes=[vctok])
    Kt = [P.sbuf([128, max(NKTMAX * 128, S)], BF16, f"Kt{i}", dma=True) for i in range(2)]
    Vt = [P.sbuf([128, max(NKTMAX, 64), 128], BF16, f"Vt{i}", dma=True) for i in range(2)]
    Kw = [P.sbuf([128, 640], BF16, f"Kw{i}", dma=True) for i in range(2)]
    Vw = [P.sbuf([128, 5, 128], BF16, f"Vw{i}", dma=True) for i in range(2)]
    q32 = [P.sbuf([128, 8, 128], F32, f"q32_{i}", dma=True) for i in range(2)]
    qb = [P.sbuf([128, 1024], BF16, f"qb{i}") for i in range(2)]
    gsb = P.sbuf([128, 128], F32, "gsb", dma=True)
    cmask = P.sbuf([128, 4, 128], F32, "cmask")
    cm = P.sbuf([128, 16, 128], BF16, "cm")
    wmask = P.sbuf([128, 5, 128], BF16, "wmask")
    negi = P.sbuf([128, 128], F32, "negi")
    pc = [[P.sbuf([128, 512], F32, f"pc{a}_{c}") for c in range(4)] for a in range(2)]
    Pb = [P.sbuf([128, 512], BF16, f"Pb{i}") for i in range(3)]
    Pmb = [P.sbuf([128, 512], BF16, f"Pmb{i}") for i in range(3)]
    msb = [P.sbuf([128, 128], BF16, f"msb{i}") for i in range(2)]
    rrec = P.sbuf([128, 512], F32, "rrec")
    wgt = P.sbuf([128, 512], F32, "wgt")
    tmpo = P.sbuf([128, 512], F32, "tmpo")
    oacc = [P.sbuf([128, 512], F32, f"oacc{i}") for i in range(2)]
    esel = [P.sbuf([96, 128], F32, f"esel{i}") for i in range(2)]
    sc = P.sbuf([128, 128], F32, "sc")
    sc2 = P.sbuf([128, 128], F32, "sc2")
    m8a = P.sbuf([128, 8], F32, "m8a")
    m8b = P.sbuf([128, 8], F32, "m8b")
    selm = P.sbuf([128, 128], BF16, "selm")
    selexp = P.sbuf([128, NKTMAX * 128], BF16, "selexp")
    hid = P.sbuf([128, 128], BF16, "hid")
    w2b = [P.sbuf([128, 128], BF16, f"w2b{i}", dma=True) for i in range(2)]
    peb = [P.sbuf([128, 32], BF16, f"peb{i}", dma=True) for i in range(2)]
    peterm = [P.sbuf([128, 1], F32, f"peterm{i}") for i in range(2)]
    sqc = P.sbuf([128, 128], F32, "sqc")
    rkc = P.sbuf([128, 128], F32, "rkc")
    vcT = P.sbuf([128, 128], F32, "vcT")
    t1 = P.sbuf([32, 128], F32, "t1c")
    t2 = P.sbuf([32, 128], F32, "t2c")
    ps = [P.psum([128, 512], F32, f"psb{i}") for i in range(8)]
    b6b = ps[6].h.ap().bitcast(BF16)
    selT = [Buf(ps[6].h, "selT0"), Buf(ps[6].h, "selT1")]
    impB = Buf(ps[6].h, "impB")

    W1 = Vt[1]
    W1f = W1.h.ap().rearrange("p a b -> p (a b)")
    for kv in range(2):
        P.dma("pool", W1, W1f[:, kv * 4096:(kv + 1) * 4096], w1d[kv], w1d[kv].ap(), max_dma_last_dim=8192)
        P.dma("pool", w2b[kv], w2b[kv][:, :], w2d[kv], w2d[kv].ap())
        P.dma("pool", peb[kv], peb[kv][:, :], ped[kv], ped[kv].ap())
        for l in range(32):
            P.op("pe", lambda e, kv=kv, l=l: e.matmul(ps[7][:, 0:1], lhsT=W1f[:, kv * 4096 + l * 128: kv * 4096 + (l + 1) * 128],
                                                      rhs=peb[kv][:, l:l + 1], start=(l == 0), stop=(l == 31)),
                 reads=[W1, peb[kv]], writes=[ps[7]])
        P.op("act", lambda e, kv=kv: e.activation(out=peterm[kv][:, :], in_=ps[7][:, 0:1], func=AF.Copy), reads=[ps[7]], writes=[peterm[kv]])
    craw = Kt[1]
    for kv in range(2):
        src = kcraw if kv == 0 else vcraw
        for g in range(4):
            P.dma("sp", craw, craw[:, 0:S], src, src.ap()[g], nowaw=False)
            for ct in range(NCT):
                c0 = ct * 128
                nblk = min(128, NCMP - c0)
                for l in range(32):
                    st = 16 * c0 + l
                    P.op("pe", lambda e, kv=kv, l=l, st=st, nblk=nblk: e.matmul(
                        ps[0][:, 0:nblk], lhsT=W1f[:, kv * 4096 + l * 128: kv * 4096 + (l + 1) * 128],
                        rhs=craw[:, st:st + 16 * (nblk - 1) + 1:16], start=(l == 0), stop=(l == 31)),
                        reads=[W1, craw], writes=[ps[0]])
                P.op("act", lambda e, kv=kv, nblk=nblk: e.activation(out=hid[:, 0:nblk], in_=ps[0][:, 0:nblk], func=AF.Silu,
                                                                     bias=peterm[kv][:, 0:1]),
                     reads=[ps[0], peterm[kv]], writes=[hid])
                P.op("pe", lambda e, kv=kv, nblk=nblk: e.matmul(ps[1][:, 0:nblk], lhsT=w2b[kv][:, :], rhs=hid[:, 0:nblk],
                                                                start=True, stop=True),
                     reads=[w2b[kv], hid], writes=[ps[1]])
                if kv == 0:
                    dst = kcT[:, g, c0:c0 + nblk]
                    P.op("act", lambda e, nblk=nblk: e.activation(out=sqc[:, 0:nblk], in_=ps[1][:, 0:nblk], func=AF.Square),
                         reads=[ps[1]], writes=[sqc])
                    P.op("pe", lambda e, nblk=nblk: e.matmul(ps[2][:, 0:nblk], lhsT=ones[:, :], rhs=sqc[:, 0:nblk], start=True, stop=True),
                         reads=[ones, sqc], writes=[ps[2]])
                    P.op("act", lambda e, nblk=nblk: e.activation(out=rkc[:, 0:nblk], in_=ps[2][:, 0:nblk], func=AF.Sqrt,
                                                                  scale=1.0 / 128, bias=epsT[:, 0:1]),
                         reads=[ps[2], epsT], writes=[rkc])
                    P.op("dve", lambda e, nblk=nblk: e.reciprocal(out=rkc[:, 0:nblk], in_=rkc[:, 0:nblk]), reads=[rkc], writes=[rkc])
                    P.op("dve", lambda e, nblk=nblk, dst=dst: e.scalar_tensor_tensor(
                        out=dst, in0=ps[1][:, 0:nblk], scalar=gkc[:, 0:1], op0=ALU.mult, in1=rkc[:, 0:nblk], op1=ALU.mult),
                        reads=[ps[1], gkc, rkc], writes=[kcT])
                    P.op("pe", lambda e, nblk=nblk, dst=dst: e.matmul(ps[2][0:32, 0:nblk], lhsT=rpm[:, :], rhs=dst[0:32, :],
                                                                      start=True, stop=True),
                         reads=[rpm, kcT], writes=[ps[2]])
                    P.op("dve", lambda e, nblk=nblk, dst=dst, c0=c0: e.tensor_tensor(out=t1[:, 0:nblk], in0=dst[0:32, :],
                                                                                   in1=rcc[:, c0:c0 + nblk], op=ALU.mult),
                         reads=[kcT, rcc], writes=[t1])
                    P.op("dve", lambda e, nblk=nblk, c0=c0: e.tensor_tensor(out=t2[:, 0:nblk], in0=ps[2][0:32, 0:nblk],
                                                                          in1=rsc[:, c0:c0 + nblk], op=ALU.mult),
                         reads=[ps[2], rsc], writes=[t2])
                    P.op("dve", lambda e, nblk=nblk, dst=dst: e.tensor_tensor(out=dst[0:32, :], in0=t1[:, 0:nblk], in1=t2[:, 0:nblk], op=ALU.add),
                         reads=[t1, t2], writes=[kcT])
                else:
                    P.op("act", lambda e, nblk=nblk: e.activation(out=vcT[:, 0:nblk], in_=ps[1][:, 0:nblk], func=AF.Copy),
                         reads=[ps[1]], writes=[vcT])
                    P.op("pe", lambda e, nblk=nblk: e.transpose(out=ps[2][0:nblk, 0:128], in_=vcT[:, 0:nblk], identity=I128[:, :]),
                         reads=[vcT, I128], writes=[ps[2]])
                    P.op("act", lambda e, nblk=nblk, g=g, ct=ct: e.activation(out=vctok[0:nblk, g, ct, :], in_=ps[2][0:nblk, 0:128], func=AF.Copy),
                         reads=[ps[2]], writes=[vctok])

    if dbg:
        dk = P.dram("dbg_kc", [128, 2048], F32, kind="ExternalOutput")
        dv = P.dram("dbg_vc", [128, 2048], F32, kind="ExternalOutput")
        P.dma("sp", dk, dk.ap(), kcT, kcT[:, :, :].rearrange("p a b -> p (a b)"))
        P.dma("sp", dv, dv.ap(), vctok, vctok[:, :, :, :].rearrange("p a b c -> p (a b c)"))
    qv = qT.ap().rearrange("(h d) t -> d h t", d=128)
    ov = oT.ap().rearrange("(h d) t -> d h t", d=128)
    pbi = [0]

    def branch_step(quad, Ksrc, Kbufs, Vsrc, Vbufs, mask_ap, mask_bufs, qbt, Oa, Ra, first, last):
        sb = ps[pbi[0] % 2]
        k = pbi[0] % 3
        pbi[0] += 1
        P.op("pe", lambda e: e.matmul(sb[:, :], lhsT=Ksrc, rhs=qbt[:, quad * 512:(quad + 1) * 512], start=True, stop=True),
             reads=list(Kbufs) + [qbt], writes=[sb])
        P.op("act", lambda e: e.activation(out=Pb[k][:, :], in_=sb[:, :], func=AF.Exp, scale=SCALE), reads=[sb], writes=[Pb[k]])
        P.op("dve", lambda e: e.tensor_tensor(out=Pmb[k][:, :].rearrange("p (h q) -> p h q", q=128),
                                              in0=Pb[k][:, :].rearrange("p (h q) -> p h q", q=128),
                                              in1=mask_ap.unsqueeze(1).to_broadcast([128, 4, 128]), op=ALU.mult),
             reads=[Pb[k]] + list(mask_bufs), writes=[Pmb[k]])
        P.op("pe", lambda e: e.matmul(Oa[:, :], lhsT=Vsrc, rhs=Pmb[k][:, :], start=first, stop=last),
             reads=list(Vbufs) + [Pmb[k]], writes=[Oa])
        P.op("pe", lambda e: e.matmul(Ra[:, :], lhsT=onesb[:, :], rhs=Pmb[k][:, :], start=first, stop=last),
             reads=[onesb, Pmb[k]], writes=[Ra])

    def finish_branch(g, quad, c, Oa, Ra, firstacc):
        if c not in branches:
            if firstacc:
                P.op("dve", lambda e: e.memset(oacc[quad][:, :], 0.0), writes=[oacc[quad]])
                P.op("dve", lambda e: e.tensor_scalar(out=rrec[:, :], in0=Ra[:, :], scalar1=1e-30, scalar2=None, op0=ALU.max),
                     reads=[Ra], writes=[rrec])
                P.op("dve", lambda e: e.reciprocal(out=rrec[:, :], in_=rrec[:, :]), reads=[rrec], writes=[rrec])
            return
        P.op("dve", lambda e: e.tensor_scalar(out=rrec[:, :], in0=Ra[:, :], scalar1=1e-30, scalar2=None, op0=ALU.max),
             reads=[Ra], writes=[rrec])
        P.op("dve", lambda e: e.reciprocal(out=rrec[:, :], in_=rrec[:, :]), reads=[rrec], writes=[rrec])
        for h in range(4):
            j = (g * 8 + quad * 4 + h) * 3 + c
            eb = esel[h % 2]
            P.op("dve", lambda e, j=j, eb=eb: e.tensor_copy(out=eb[:, :], in_=I128[0:96, j:j + 1].to_broadcast([96, 128])),
                 reads=[I128], writes=[eb])
            P.op("pe", lambda e, h=h, eb=eb: e.matmul(ps[7][:, h * 128:(h + 1) * 128], lhsT=eb[:, :], rhs=gsb[0:96, :],
                                                      start=True, stop=True),
                 reads=[eb, gsb], writes=[ps[7]])
        P.op("dve", lambda e: e.tensor_tensor(out=wgt[:, :], in0=ps[7][:, :], in1=rrec[:, :], op=ALU.mult),
             reads=[ps[7], rrec], writes=[wgt])
        if firstacc:
            P.op("dve", lambda e: e.tensor_tensor(out=oacc[quad][:, :], in0=Oa[:, :], in1=wgt[:, :], op=ALU.mult),
                 reads=[Oa, wgt], writes=[oacc[quad]])
        else:
            P.op("dve", lambda e: e.tensor_tensor(out=tmpo[:, :], in0=Oa[:, :], in1=wgt[:, :], op=ALU.mult),
                 reads=[Oa, wgt], writes=[tmpo])
            P.op("dve", lambda e: e.tensor_tensor(out=oacc[quad][:, :], in0=oacc[quad][:, :], in1=tmpo[:, :], op=ALU.add),
                 reads=[oacc[quad], tmpo], writes=[oacc[quad]])

    it = 0
    for i in range(NB):
        base = (i // 2) * 16
        n_kt = min(8 * (i + 1), S // 128)
        qs = slice(i * 128, (i + 1) * 128)
        P.dma("sp", gsb, gsb[:, :], gT, gT.ap()[:, qs], nowaw=False)
        for ct in range(NCT):
            P.op("dve", lambda e, ct=ct, qs=qs: e.tensor_scalar(out=cmask[:, ct, :], in0=qpos[:, qs], scalar1=cend[:, ct:ct + 1],
                                                              scalar2=None, op0=ALU.is_ge),
                 reads=[qpos, cend], writes=[cmask])
        for kt in range(base, n_kt):
            P.op("dve", lambda e, kt=kt, qs=qs, base=base: e.tensor_scalar(out=cm[:, kt - base, :], in0=qpos[:, qs],
                                                                         scalar1=kpos[:, kt:kt + 1], scalar2=None, op0=ALU.is_ge),
                 reads=[qpos, kpos], writes=[cm])
        for w in range(5):
            P.op("dve", lambda e, w=w, i=i: e.tensor_scalar(out=wmask[:, w, :], in0=wm[:, w * 128:(w + 1) * 128],
                                                          scalar1=kval[:, i * 5 + w:i * 5 + w + 1], scalar2=None, op0=ALU.mult),
                 reads=[wm, kval], writes=[wmask])
        P.op("dve", lambda e, qs=qs: e.tensor_scalar(out=negi[:, :], in0=valid[:, qs], scalar1=-1.0, scalar2=-NEGV,
                                                   op0=ALU.add, op1=ALU.mult),
             reads=[valid], writes=[negi])
        for g in range(4):
            b = it % 2
            it += 1
            P.dma("sp", q32[b], q32[b][:, :, :], qT, qv[:, g * 8:(g + 1) * 8, qs], nowaw=False)
            P.dma("sp", Kt[b], Kt[b][:, 0:n_kt * 128], ksT, ksT.ap()[g][:, 0:n_kt * 128], nowaw=False)
            P.dma("sp", Vt[b], Vt[b][:, 0:n_kt, :], vsl, vsl.ap()[g][:, 0:n_kt, :], nowaw=False)
            P.dma("sp", Kw[b], Kw[b][:, :], kwT, kwT.ap()[g, i], nowaw=False)
            P.dma("sp", Vw[b], Vw[b][:, :, :], vwl, vwl.ap()[g, i], nowaw=False)
            P.op("act", lambda e, b=b: e.activation(out=qb[b][:, :], in_=q32[b][:, :, :].rearrange("p h q -> p (h q)"), func=AF.Copy),
                 reads=[q32[b]], writes=[qb[b]])
            q32f = q32[b].h.ap().rearrange("p h q -> p (h q)")
            nimp = 2 * NCT * 4
            ii = 0
            for quad in range(2):
                for ct in range(NCT):
                    sb = ps[pbi[0] % 2]
                    pbi[0] += 1
                    pt = pc[quad][ct]
                    P.op("pe", lambda e, sb=sb, ct=ct, quad=quad, g=g, q32f=q32f: e.matmul(
                        sb[:, :], lhsT=kcT[:, g, ct * 128:(ct + 1) * 128], rhs=q32f[:, quad * 512:(quad + 1) * 512], start=True, stop=True),
                        reads=[kcT, q32[b]], writes=[sb])
                    P.op("act", lambda e, sb=sb, pt=pt: e.activation(out=pt[:, :], in_=sb[:, :], func=AF.Exp, scale=SCALE),
                         reads=[sb], writes=[pt])
                    P.op("dve", lambda e, pt=pt, ct=ct: e.tensor_tensor(
                        out=pt[:, :].rearrange("p (h q) -> p h q", q=128), in0=pt[:, :].rearrange("p (h q) -> p h q", q=128),
                        in1=cmask[:, ct, :].unsqueeze(1).to_broadcast([128, 4, 128]), op=ALU.mult),
                        reads=[pt, cmask], writes=[pt])
                    P.op("pe", lambda e, pt=pt, ct=ct, g=g: e.matmul(ps[2][:, :], lhsT=vctok[:, g, ct, :], rhs=pt[:, :],
                                                                      start=(ct == 0), stop=(ct == NCT - 1)),
                         reads=[vctok, pt], writes=[ps[2]])
                    P.op("pe", lambda e, pt=pt, ct=ct: e.matmul(ps[3][:, :], lhsT=ones[:, :], rhs=pt[:, :],
                                                                start=(ct == 0), stop=(ct == NCT - 1)),
                         reads=[ones, pt], writes=[ps[3]])
                finish_branch(g, quad, 0, ps[2], ps[3], True)
                for ct in range(NCT):
                    pt = pc[quad][ct]
                    P.op("dve", lambda e, pt=pt: e.tensor_tensor(out=pt[:, :], in0=pt[:, :], in1=rrec[:, :], op=ALU.mult),
                         reads=[pt, rrec], writes=[pt])
                    for h in range(4):
                        P.op("pe", lambda e, pt=pt, h=h, ct=ct, ii=ii: e.matmul(
                            ps[6][:, 256:384], lhsT=pt[:, h * 128:(h + 1) * 128], rhs=Msel[:, ct * 128:(ct + 1) * 128],
                            start=(ii == 0), stop=(ii == nimp - 1)),
                            reads=[pt, Msel], writes=[impB])
                        ii += 1
            P.op("dve", lambda e, qs=qs: e.tensor_tensor(out=sc[:, :], in0=ps[6][:, 256:384], in1=bonus[:, qs], op=ALU.add),
                 reads=[impB, bonus], writes=[sc])
            P.op("dve", lambda e, qs=qs: e.tensor_tensor(out=sc[:, :], in0=sc[:, :], in1=valid[:, qs], op=ALU.mult),
                 reads=[sc, valid], writes=[sc])
            P.op("dve", lambda e: e.tensor_tensor(out=sc[:, :], in0=sc[:, :], in1=negi[:, :], op=ALU.add),
                 reads=[sc, negi], writes=[sc])
            P.op("dve", lambda e: e.max(out=m8a[:, :], in_=sc[:, :]), reads=[sc], writes=[m8a])
            P.op("dve", lambda e: e.match_replace(out=sc2[:, :], in_to_replace=m8a[:, :], in_values=sc[:, :], imm_value=-3.0e38),
                 reads=[sc, m8a], writes=[sc2])
            P.op("dve", lambda e: e.max(out=m8b[:, :], in_=sc2[:, :]), reads=[sc2], writes=[m8b])
            P.op("dve", lambda e: e.tensor_scalar(out=selm[:, :], in0=sc[:, :], scalar1=m8b[:, 7:8], scalar2=None, op0=ALU.is_ge),
                 reads=[sc, m8b], writes=[selm])
            P.op("dve", lambda e, n_kt=n_kt: e.tensor_copy(
                out=selexp[:, 0:n_kt * 128].rearrange("p (j k) -> p j k", k=64),
                in_=selm[:, 0:2 * n_kt].unsqueeze(2).to_broadcast([128, 2 * n_kt, 64])),
                reads=[selm], writes=[selexp])
            for kt in range(n_kt):
                sb_ = selT[kt % 2]
                sap = b6b[:, (kt % 2) * 128:(kt % 2 + 1) * 128]
                P.op("pe", lambda e, kt=kt, sap=sap: e.transpose(
                    out=sap, in_=selexp[:, kt * 128:(kt + 1) * 128], identity=identb[:, :]),
                    reads=[selexp, identb], writes=[sb_])
                if kt >= base:
                    mb = msb[kt % 2]
                    P.op("dve", lambda e, sap=sap, kt=kt, mb=mb, base=base: e.tensor_tensor(out=mb[:, :], in0=sap, in1=cm[:, kt - base, :],
                                                                                            op=ALU.mult),
                         reads=[sb_, cm], writes=[mb])
                    mask_ap, mask_bufs = mb[:, :], [mb]
                else:
                    mask_ap, mask_bufs = sap, [sb_]
                for quad in range(2):
                    branch_step(quad, Kt[b][:, kt * 128:(kt + 1) * 128], [Kt[b]], Vt[b][:, kt, :], [Vt[b]], mask_ap, mask_bufs,
                                qb[b], ps[2 + 2 * quad], ps[3 + 2 * quad], kt == 0, kt == n_kt - 1)
            for quad in range(2):
                finish_branch(g, quad, 1, ps[2 + 2 * quad], ps[3 + 2 * quad], False)
            for w in range(5):
                for quad in range(2):
                    branch_step(quad, Kw[b][:, w * 128:(w + 1) * 128], [Kw[b]], Vw[b][:, w, :], [Vw[b]], wmask[:, w, :], [wmask],
                                qb[b], ps[2 + 2 * quad], ps[3 + 2 * quad], w == 0, w == 4)
            for quad in range(2):
                finish_branch(g, quad, 2, ps[2 + 2 * quad], ps[3 + 2 * quad], False)
                P.dma("sp", oT, ov[:, g * 8 + quad * 4:g * 8 + quad * 4 + 4, qs], oacc[quad],
                      oacc[quad][:, :].rearrange("p (h q) -> p h q", q=128))
    P.wait_all("sp", [oT])
    P.emit()
    return nc


def d2_const_inputs(S, cmp_w1_k, cmp_w2_k, cmp_pe_k, cmp_w1_v, cmp_w2_v, cmp_pe_v, k_norm_cmp):
    f = np.ascontiguousarray
    ncmp = S // 16 - 1
    c = np.arange(512)
    rc, rs = rope_tables(np.where(c < ncmp, 16 * c + 31, 0))
    Msel = np.zeros((4, 128, 128), np.float32)
    for j in range(128):
        for off, wgt in ((-1, 0.5), (0, 1.0), (1, 1.0), (2, 1.0), (3, 0.5)):
            cc = 4 * j + off
            if 0 <= cc < min(ncmp, 512):
                Msel[cc // 128, cc % 128, j] = wgt
    cend = np.where(c < ncmp, 16.0 * c + 31.0, 1e9).astype(np.float32).reshape(4, 128).T
    kpos = (np.arange(64)[None, :] * 128 + np.arange(128)[:, None]).astype(np.float32)
    k = np.arange(128)[:, None]
    q = np.arange(128)[None, :]
    wm = np.concatenate([((k <= q + 512 - 128 * w) & (k + 128 * w > q)).astype(np.float32) for w in range(5)], axis=1)

    def w1l(w1):
        return f(w1.reshape(32, 128, 128).transpose(1, 0, 2).reshape(128, 4096))
    return {
        "w1k": w1l(cmp_w1_k), "w1v": w1l(cmp_w1_v), "w2k": f(cmp_w2_k), "w2v": f(cmp_w2_v),
        "pekT": f(cmp_pe_k.T), "pevT": f(cmp_pe_v.T), "g_kc": f(k_norm_cmp.reshape(128, 1)),
        "ropeCc": rc, "ropeSc": rs, "ropeP": rope_perm(), "I128": np.eye(128, dtype=np.float32),
        "Msel": f(Msel.transpose(1, 0, 2).reshape(128, 512)), "cend_col": f(cend), "kpos_col": f(kpos), "wm": f(wm),
    }


def d2_kv_inputs(kv_full, S):
    f = np.ascontiguousarray
    kv6 = kv_full.reshape(S, 6, 4, 128)
    return {
        "kcrawT": f(kv6[:, 0].transpose(1, 2, 0)), "vcrawT": f(kv6[:, 1].transpose(1, 2, 0)),
        "ksT": f(kv6[:, 2].transpose(1, 2, 0)),
        "vs_l": f(kv6[:, 3].reshape(S // 128, 128, 4, 128).transpose(2, 1, 0, 3)),
    }, kv6


def d2_core_inputs(kv6, S, blocks, pos):
    f = np.ascontiguousarray
    NB = len(blocks)
    TC = NB * 128
    zpad = np.zeros((512, 4, 128), kv6.dtype)
    kwp = np.concatenate([zpad, kv6[:, 4]], 0)
    vwp = np.concatenate([zpad, kv6[:, 5]], 0)
    kwT = np.stack([kwp[gb * 128:gb * 128 + 640].transpose(1, 2, 0) for gb in blocks], 1)
    vwl = np.stack([vwp[gb * 128:gb * 128 + 640].reshape(5, 128, 4, 128).transpose(2, 1, 0, 3) for gb in blocks], 1)
    kval = np.zeros((128, NB * 5), np.float32)
    for i, gb in enumerate(blocks):
        for w in range(5):
            kval[:, i * 5 + w] = (gb * 128 - 512 + w * 128 + np.arange(128) >= 0)
    qpos = np.broadcast_to(pos.astype(np.float32)[None, :], (128, TC))
    j = np.arange(128)[None, :]
    bonus = np.zeros((128, TC), np.float32)
    valid = np.zeros((128, TC), np.float32)
    for i in range(NB):
        t = pos[i * 128:(i + 1) * 128][:, None]
        cur = t // 64
        forced = (j == 0) | (j == cur) | (j == cur - 1)
        bonus[:, i * 128:(i + 1) * 128] = np.where(forced, 1e4, 0.0)
        valid[:, i * 128:(i + 1) * 128] = (j <= cur)
    return {"kwT": f(kwT), "vw_l": f(vwl), "kvalid_col": kval, "qpos_bc": f(qpos), "bonusT": bonus, "validT": valid}


def _run(nc, in_maps):
    res = run_bass_kernel_spmd(nc, in_maps, core_ids=list(range(NCORE)))
    return [{k: np.asarray(v) for k, v in r.items()} for r in res.results]


def kernel(**inputs):
    I = {k: np.asarray(v) for k, v in inputs.items()}
    x = I["x"][0]
    S = x.shape[0]
    TC = S // NCORE
    tix = token_index(S)
    cb = core_blocks(S)
    f = np.ascontiguousarray

    Wi = lay_w(I["ffn_a_w_in"][0]); Wo = lay_w(I["ffn_a_w_out"][0]); Ws = lay_w(I["ssm_w_in"][0])
    g1 = lay_vec(I["ffn_a_norm"][0]); g2 = lay_vec(I["mix_norm"][0])
    ra = _run(build_stage_a(TC), [{"xT": f(x[tix[c]].T), "w_in": Wi, "w_out": Wo, "ssm_w_in": Ws, "g_ffn": g1, "g_mix": g2}
                                  for c in range(NCORE)])
    del Wi, Wo, Ws
    zx = np.empty((S, SSM_COLS), np.float32)
    for c in range(NCORE):
        zx[tix[c]] = ra[c]["zxT"].T
    h1T = [ra[c]["h1T"] for c in range(NCORE)]
    del ra

    sargs = [I[k][0] for k in ["ssm_conv_w", "ssm_conv_b", "ssm_dt_bias", "ssm_a_log", "ssm_d", "ssm_norm"]]
    rb = _run(build_stage_b(S), [stage_b_inputs(zx, g, *sargs) for g in range(NCORE)])
    del zx
    y = np.empty((S, 8192), np.float32)
    for g in range(NCORE):
        y[:, g * 1024:(g + 1) * 1024] = rb[g]["yT"].T
    del rb

    Wso = lay_w(I["ssm_w_out"][0]); Wkv = lay_w(I["kv_w"])
    Wi = lay_w(I["ffn_b_w_in"][0]); Wo = lay_w(I["ffn_b_w_out"][0])
    gk = f(np.stack([I["k_norm_slc"], I["k_norm_win"]], 1))
    maps = []
    ropes = [rope_tables(tix[c]) for c in range(NCORE)]
    for c in range(NCORE):
        maps.append({"yT": f(y[tix[c]].T), "h1T": h1T[c], "w_so": Wso, "w_kv": Wkv, "g_kv": lay_vec(I["kv_norm"]), "g_k": gk,
                     "ropeC": ropes[c][0], "ropeS": ropes[c][1], "ropeP": rope_perm(),
                     "w_in": Wi, "w_out": Wo, "g_ffn": lay_vec(I["ffn_b_norm"][0])})
    rc = _run(build_stage_c(TC), maps)
    del Wso, Wkv, Wi, Wo, maps, y, h1T
    kv_full = np.empty((S, 3072), rc[0]["kvT"].dtype)
    for c in range(NCORE):
        kv_full[tix[c]] = rc[c]["kvT"].T
    h3T = [rc[c]["h3T"] for c in range(NCORE)]
    del rc

    wqg = I["attn_w_qg"][0]
    wq = np.zeros((D, 33 * 128), np.float32)
    wq[:, :wqg.shape[1]] = wqg
    Wq = lay_w(wq)
    Wi = lay_w(I["ffn_a_w_in"][1]); Wo = lay_w(I["ffn_a_w_out"][1])
    maps = [{"h3T": h3T[c], "w_q": Wq, "g_mix": lay_vec(I["mix_norm"][1]), "g_q": f(I["attn_q_norm"][0].reshape(128, 1)),
             "ropeC": ropes[c][0], "ropeS": ropes[c][1], "ropeP": rope_perm(),
             "w_in": Wi, "w_out": Wo, "g_ffn": lay_vec(I["ffn_a_norm"][1])} for c in range(NCORE)]
    rd1 = _run(build_stage_d1(TC), maps)
    del Wq, Wi, Wo, maps, h3T

    cst = d2_const_inputs(S, I["cmp_w1_k"], I["cmp_w2_k"], I["cmp_pe_k"], I["cmp_w1_v"], I["cmp_w2_v"], I["cmp_pe_v"], I["k_norm_cmp"])
    shared, kv6 = d2_kv_inputs(kv_full, S)
    maps = []
    for c in range(NCORE):
        m = dict(cst)
        m.update(shared)
        m.update(d2_core_inputs(kv6, S, cb[c], tix[c]))
        m["qT"] = rd1[c]["qT"]
        m["gT"] = rd1[c]["gT"]
        maps.append(m)
    rd2 = _run(build_stage_d2(S, TC), maps)
    del maps

    Wao = lay_w(I["attn_w_o"][0])
    Wi = lay_w(I["ffn_b_w_in"][1]); Wo = lay_w(I["ffn_b_w_out"][1])
    maps = [{"oT": rd2[c]["oT"], "h4T": rd1[c]["h4T"], "w_ao": Wao, "w_in": Wi, "w_out": Wo,
             "g_ffn": lay_vec(I["ffn_b_norm"][1])} for c in range(NCORE)]
    rd3 = _run(build_stage_d3(TC), maps)
    out = np.empty((1, S, D), np.float32)
    for c in range(NCORE):
        out[0, tix[c]] = rd3[c]["outT"].T
    return out
```
